# Optimizing a Trainium2 kernel written in Bass

```python
import jax, jax.numpy as jnp
from jax import lax
import numpy as np

D_MODEL = 4096
BATCH = 1
SEQ = 8192
DEPTH = 4

N_MIXERS = 4
NORM_EPS = 1e-6
LN_EPS = 1e-5
L2_EPS = 1e-6
ROPE_THETA = 10000.0
ATT_HEAD_DIM = 128
ATT_HEADS = D_MODEL // ATT_HEAD_DIM
ATT_KV_HEADS = ATT_HEADS // 4
WINDOW = 128
BLOCK = WINDOW
GDN_HEAD_DIM = 128
GDN_QK_HEADS = D_MODEL // (2 * GDN_HEAD_DIM)
GDN_V_HEADS = D_MODEL // GDN_HEAD_DIM
GDN_KEY_DIM = GDN_QK_HEADS * GDN_HEAD_DIM
GDN_VAL_DIM = GDN_V_HEADS * GDN_HEAD_DIM
GDN_CONV = 4
GDN_CHUNK = 64
CONF_KERNEL = 31
SHORT_KERNEL = 3
D_FF = ((8 * D_MODEL // 3 + 255) // 256) * 256
FFN_KERNEL = 3

kernel_name = 'hybrid_swa_gdn_conformer_shortconv_trunk'

F32 = jnp.float32


def _layers_of(kind):
    return len(range(kind, DEPTH, N_MIXERS))


def rms_norm(x, g):
    xf = x.astype(F32)
    y = xf * lax.rsqrt(jnp.mean(xf * xf, axis=-1, keepdims=True) + NORM_EPS)
    return (y * g.astype(F32)).astype(x.dtype)


def layer_norm(x, g, b):
    xf = x.astype(F32)
    xc = xf - jnp.mean(xf, axis=-1, keepdims=True)
    var = jnp.mean(xc * xc, axis=-1, keepdims=True)
    return (xc * lax.rsqrt(var + LN_EPS) * g.astype(F32) + b.astype(F32)).astype(x.dtype)


def l2_norm(x):
    xf = x.astype(F32)
    return xf * lax.rsqrt(jnp.sum(xf * xf, axis=-1, keepdims=True) + L2_EPS)


def causal_dwconv(x, w):
    K = w.shape[0]
    S = x.shape[1]
    xp = jnp.pad(x, ((0, 0), (K - 1, 0), (0, 0)))
    y = xp[:, 0:S] * w[0]
    for k in range(1, K):
        y = y + xp[:, k:k + S] * w[k]
    return y


def rope(x, pos):
    half = x.shape[-1] // 2
    inv = jnp.power(ROPE_THETA, -jnp.arange(half, dtype=F32) / half)
    ang = pos.astype(F32)[:, None] * inv[None, :]
    cos = jnp.cos(ang)[None, :, None, :]
    sin = jnp.sin(ang)[None, :, None, :]
    xf = x.astype(F32)
    x1, x2 = xf[..., :half], xf[..., half:]
    return jnp.concatenate([x1 * cos - x2 * sin, x2 * cos + x1 * sin], axis=-1).astype(x.dtype)


def swa_sink_attention(h, w_qkv, w_o, sinks):
    B, S, _ = h.shape
    H, KV, D = ATT_HEADS, ATT_KV_HEADS, ATT_HEAD_DIM
    G = H // KV
    NB = S // BLOCK
    q, k, v = jnp.split(h @ w_qkv, [H * D, H * D + KV * D], axis=-1)
    pos = jnp.arange(S)
    q = rope(q.reshape(B, S, H, D), pos)
    k = rope(k.reshape(B, S, KV, D), pos)
    v = v.reshape(B, S, KV, D)
    qb = q.reshape(B, NB, BLOCK, KV, G, D)

    def band(t):
        tp = jnp.pad(t, ((0, 0), (BLOCK, 0), (0, 0), (0, 0))).reshape(B, NB + 1, BLOCK, KV, D)
        return jnp.concatenate([tp[:, :-1], tp[:, 1:]], axis=2)

    kb, vb = band(k), band(v)
    s = jnp.einsum('bnqhgd,bnkhd->bnhgqk', qb, kb, preferred_element_type=F32) * (D ** -0.5)
    qi = jnp.arange(BLOCK)[:, None]
    kj = jnp.arange(2 * BLOCK)[None, :]
    rel = BLOCK + qi - kj
    in_win = (rel >= 0) & (rel < WINDOW)
    kpos = jnp.arange(NB)[:, None] * BLOCK - BLOCK + kj
    mask = in_win[None, :, :] & (kpos >= 0)[:, None, :]
    s = jnp.where(mask[None, :, None, None, :, :], s, -jnp.inf)
    sink = sinks.astype(F32).reshape(KV, G)[None, None, :, :, None, None]
    m = jnp.maximum(jnp.max(s, axis=-1, keepdims=True), sink)
    p = jnp.exp(s - m)
    denom = jnp.sum(p, axis=-1, keepdims=True) + jnp.exp(sink - m)
    o = jnp.einsum('bnhgqk,bnkhd->bnqhgd', (p / denom).astype(vb.dtype), vb)
    return o.reshape(B, S, H * D) @ w_o


def chunk_gated_delta_rule(q, k, v, g, beta):
    B, S, H, Dk = k.shape
    Dv = v.shape[-1]
    C = GDN_CHUNK
    N = S // C

    def chunks(t):
        return jnp.moveaxis(t.astype(F32).reshape(B, N, C, H, t.shape[-1]), 3, 2)

    qc = chunks(q) * (Dk ** -0.5)
    kc, vc = chunks(k), chunks(v)
    gc = jnp.moveaxis(g.astype(F32).reshape(B, N, C, H), 3, 2)
    bc = jnp.moveaxis(beta.astype(F32).reshape(B, N, C, H), 3, 2)
    gcum = jnp.cumsum(gc, axis=-1)
    tril = jnp.tril(jnp.ones((C, C), bool))
    strict = jnp.tril(jnp.ones((C, C), bool), -1)
    decay = jnp.exp(jnp.where(tril, gcum[..., :, None] - gcum[..., None, :], -jnp.inf))
    kbeta = kc * bc[..., None]
    vbeta = vc * bc[..., None]
    a_kk = jnp.einsum('bnhid,bnhjd->bnhij', kbeta, kc) * decay
    lower = jnp.where(strict, a_kk, 0.0) + jnp.eye(C, dtype=F32)
    rhs = jnp.concatenate([vbeta, kbeta * jnp.exp(gcum)[..., None]], axis=-1)
    sol = lax.linalg.triangular_solve(lower, rhs, left_side=True, lower=True, unit_diagonal=True)
    u, w = sol[..., :Dv], sol[..., Dv:]
    attn = jnp.einsum('bnhid,bnhjd->bnhij', qc, kc) * decay
    qg = qc * jnp.exp(gcum)[..., None]
    glast = gcum[..., -1]
    kd = kc * jnp.exp(glast[..., None] - gcum)[..., None]

    def step(state, xs):
        u_n, w_n, qg_n, kd_n, attn_n, gl_n = xs
        v_new = u_n - jnp.einsum('bhck,bhkv->bhcv', w_n, state)
        out = jnp.einsum('bhck,bhkv->bhcv', qg_n, state) + jnp.einsum('bhij,bhjv->bhiv', attn_n, v_new)
        state = state * jnp.exp(gl_n)[..., None, None] + jnp.einsum('bhck,bhcv->bhkv', kd_n, v_new)
        return state, out

    xs = (jnp.moveaxis(u, 1, 0), jnp.moveaxis(w, 1, 0), jnp.moveaxis(qg, 1, 0),
          jnp.moveaxis(kd, 1, 0), jnp.moveaxis(attn, 1, 0), jnp.moveaxis(glast, 1, 0))
    state0 = jnp.zeros((B, H, Dk, Dv), F32)
    _, out = lax.scan(step, state0, xs)
    return jnp.moveaxis(out, 0, 1).transpose(0, 1, 3, 2, 4).reshape(B, S, H, Dv)


def gated_deltanet(h, w_in, conv_w, a_log, dt_bias, norm_w, w_o):
    B, S, _ = h.shape
    HK, HV, D = GDN_QK_HEADS, GDN_V_HEADS, GDN_HEAD_DIM
    n_qkv = 2 * GDN_KEY_DIM + GDN_VAL_DIM
    qkv, z, b, a = jnp.split(h @ w_in, [n_qkv, n_qkv + GDN_VAL_DIM, n_qkv + GDN_VAL_DIM + HV], axis=-1)
    qkv = jax.nn.silu(causal_dwconv(qkv, conv_w))
    q, k, v = jnp.split(qkv, [GDN_KEY_DIM, 2 * GDN_KEY_DIM], axis=-1)
    rep = HV // HK
    q = jnp.repeat(l2_norm(q.reshape(B, S, HK, D)), rep, axis=2)
    k = jnp.repeat(l2_norm(k.reshape(B, S, HK, D)), rep, axis=2)
    v = v.reshape(B, S, HV, D)
    beta = jax.nn.sigmoid(b.astype(F32))
    g = -jnp.exp(a_log.astype(F32)) * jax.nn.softplus(a.astype(F32) + dt_bias.astype(F32))
    o = chunk_gated_delta_rule(q, k, v, g, beta)
    o = rms_norm(o, norm_w) * jax.nn.silu(z.reshape(B, S, HV, D).astype(F32))
    return o.reshape(B, S, GDN_VAL_DIM).astype(h.dtype) @ w_o


def conformer_conv(h, w_pw1, b_pw1, w_dw, b_dw, ln_g, ln_b, w_pw2, b_pw2):
    val, gate = jnp.split(h @ w_pw1 + b_pw1, 2, axis=-1)
    u = val * jax.nn.sigmoid(gate)
    u = causal_dwconv(u, w_dw) + b_dw
    u = jax.nn.silu(layer_norm(u, ln_g, ln_b))
    return u @ w_pw2 + b_pw2


def short_gated_conv(h, w_in, w_conv, w_out):
    bg, cg, xin = jnp.split(h @ w_in, 3, axis=-1)
    return (bg * causal_dwconv(cg * xin, w_conv)) @ w_out


def conv_glu_ffn(h, w_gate, w_up, w_conv, b_conv, w_down):
    gt = causal_dwconv(h @ w_gate, w_conv) + b_conv
    return (jax.nn.silu(gt) * (h @ w_up)) @ w_down


def setup_inputs(seed: int = 0) -> dict:
    key = jax.random.key(seed)
    kit = iter(list(jax.random.split(key, 48)))

    def nrm(shape, scale):
        return jax.random.normal(next(kit), shape, F32) * scale

    def uni(shape, lo, hi):
        return jax.random.uniform(next(kit), shape, F32, lo, hi)

    nA, nB, nC, nD = (_layers_of(m) for m in range(N_MIXERS))
    Dm = D_MODEL
    att_in = ATT_HEADS * ATT_HEAD_DIM + 2 * ATT_KV_HEADS * ATT_HEAD_DIM
    gdn_in = 2 * GDN_KEY_DIM + 2 * GDN_VAL_DIM + 2 * GDN_V_HEADS
    gdn_conv_ch = 2 * GDN_KEY_DIM + GDN_VAL_DIM
    dt = jnp.exp(uni((nB, GDN_V_HEADS), float(np.log(1e-3)), float(np.log(1e-1))))
    return {
        'x': nrm((BATCH, SEQ, Dm), 1.0),
        'mix_norm': 1.0 + nrm((DEPTH, Dm), 0.02),
        'ffn_norm': 1.0 + nrm((DEPTH, Dm), 0.02),
        'final_norm': 1.0 + nrm((Dm,), 0.02),
        'a_w_qkv': nrm((nA, Dm, att_in), Dm ** -0.5),
        'a_w_o': nrm((nA, ATT_HEADS * ATT_HEAD_DIM, Dm), (ATT_HEADS * ATT_HEAD_DIM) ** -0.5),
        'a_sinks': nrm((nA, ATT_HEADS), 1.0),
        'b_w_in': nrm((nB, Dm, gdn_in), Dm ** -0.5),
        'b_conv': nrm((nB, GDN_CONV, gdn_conv_ch), GDN_CONV ** -0.5),
        'b_a_log': jnp.log(uni((nB, GDN_V_HEADS), 1.0, 16.0)),
        'b_dt_bias': dt + jnp.log(-jnp.expm1(-dt)),
        'b_norm': 1.0 + nrm((nB, GDN_HEAD_DIM), 0.02),
        'b_w_o': nrm((nB, GDN_VAL_DIM, Dm), GDN_VAL_DIM ** -0.5),
        'c_w_pw1': nrm((nC, Dm, 2 * Dm), Dm ** -0.5),
        'c_b_pw1': nrm((nC, 2 * Dm), 0.02),
        'c_w_dw': nrm((nC, CONF_KERNEL, Dm), CONF_KERNEL ** -0.5),
        'c_b_dw': nrm((nC, Dm), 0.02),
        'c_ln_g': 1.0 + nrm((nC, Dm), 0.02),
        'c_ln_b': nrm((nC, Dm), 0.02),
        'c_w_pw2': nrm((nC, Dm, Dm), Dm ** -0.5),
        'c_b_pw2': nrm((nC, Dm), 0.02),
        'd_w_in': nrm((nD, Dm, 3 * Dm), Dm ** -0.5),
        'd_w_conv': nrm((nD, SHORT_KERNEL, Dm), SHORT_KERNEL ** -0.5),
        'd_w_out': nrm((nD, Dm, Dm), Dm ** -0.5),
        'f_w_gate': nrm((DEPTH, Dm, D_FF), Dm ** -0.5),
        'f_w_up': nrm((DEPTH, Dm, D_FF), Dm ** -0.5),
        'f_w_conv': nrm((DEPTH, FFN_KERNEL, D_FF), FFN_KERNEL ** -0.5),
        'f_b_conv': nrm((DEPTH, D_FF), 0.02),
        'f_w_down': nrm((DEPTH, D_FF, Dm), D_FF ** -0.5),
    }


def reference(x, mix_norm, ffn_norm, final_norm,
              a_w_qkv, a_w_o, a_sinks,
              b_w_in, b_conv, b_a_log, b_dt_bias, b_norm, b_w_o,
              c_w_pw1, c_b_pw1, c_w_dw, c_b_dw, c_ln_g, c_ln_b, c_w_pw2, c_b_pw2,
              d_w_in, d_w_conv, d_w_out,
              f_w_gate, f_w_up, f_w_conv, f_b_conv, f_w_down):
    for i in range(DEPTH):
        kind, j = i % N_MIXERS, i // N_MIXERS
        h = rms_norm(x, mix_norm[i])
        if kind == 0:
            y = swa_sink_attention(h, a_w_qkv[j], a_w_o[j], a_sinks[j])
        elif kind == 1:
            y = gated_deltanet(h, b_w_in[j], b_conv[j], b_a_log[j], b_dt_bias[j], b_norm[j], b_w_o[j])
        elif kind == 2:
            y = conformer_conv(h, c_w_pw1[j], c_b_pw1[j], c_w_dw[j], c_b_dw[j],
                               c_ln_g[j], c_ln_b[j], c_w_pw2[j], c_b_pw2[j])
        else:
            y = short_gated_conv(h, d_w_in[j], d_w_conv[j], d_w_out[j])
        x = x + y
        h = rms_norm(x, ffn_norm[i])
        x = x + conv_glu_ffn(h, f_w_gate[i], f_w_up[i], f_w_conv[i], f_b_conv[i], f_w_down[i])
    return rms_norm(x, final_norm)
```

```python
import numpy as np
import concourse.bass as bass
import concourse.mybir as mybir
from concourse.bass_utils import run_bass_kernel_spmd
from contextlib import ExitStack

F32 = mybir.dt.float32
BF16 = mybir.dt.bfloat16
ALU = mybir.AluOpType
AF = mybir.ActivationFunctionType
AX = mybir.AxisListType


class Res:
    __slots__ = ("w", "r", "excl")

    def __init__(self, excl=False):
        self.w = None
        self.r = {}
        self.excl = excl


class Prog:
    ENG = ("pe", "act", "dve", "pool", "sp")

    def __init__(self, nc, es):
        self.nc = nc
        self.es = es
        self.es_sem = es
        self.n_cc = 0
        self.h = {"pe": nc.tensor, "act": nc.scalar, "dve": nc.vector, "pool": nc.gpsimd, "sp": nc.sync}
        self.sems = {}
        self.cnt = {}
        for k in self.ENG:
            self.sems[k] = es.enter_context(nc.semaphore("s_" + k))
            self.cnt[k] = 0
        self.seen = {k: {} for k in self.ENG}
        self.engkey = {k: k for k in self.ENG}
        self.gen = 0
        self.n_dma_sem = 0
        self.lastwait = {}
        self.serial = {}

    def sb(self, name, shape, dt):
        return self.es.enter_context(self.nc.sbuf_tensor(getattr(self, "prefix", "") + name, list(shape), dt))

    def ps(self, name, shape, dt=F32):
        return self.es.enter_context(self.nc.psum_tensor(name, list(shape), dt))

    def new_dma_sem(self):
        k = "dma%d" % self.n_dma_sem
        self.n_dma_sem += 1
        self.sems[k] = self.es_sem.enter_context(self.nc.semaphore(k))
        self.cnt[k] = 0
        return k

    def barrier(self):
        for e in self.ENG:
            for k, v in list(self.cnt.items()):
                if v > 0:
                    self._wait(e, (k, v))
        self.gen += 1
        for e in self.ENG:
            k = "%s_%d" % (e, self.gen)
            self.sems[k] = self.es_sem.enter_context(self.nc.semaphore("s_" + k))
            self.cnt[k] = 0
            self.engkey[e] = k

    def coll_allgather(self, ncore, reads, writes, in_ap, out_ap):
        k = "cc"
        self.n_cc += 1
        if k not in self.sems:
            self.sems[k] = self.es_sem.enter_context(self.nc.semaphore(k))
            self.cnt[k] = 0
        self._deps("pool", reads, writes)
        self.h["pool"].collective_compute("AllGather", ALU.bypass, replica_groups=[list(range(ncore))],
                                          ins=[in_ap], outs=[out_ap]).then_inc(self.sems[k])
        self.cnt[k] += 1
        self._commit((k, self.cnt[k]), reads, writes)

    def _wait(self, eng, ev):
        if ev is None:
            return
        key, val = ev
        if eng == "pe" and key.split("_")[0] == "pe":
            return
        if key.startswith("dma"):
            val = self.cnt[key]
        if self.seen[eng].get(key, 0) >= val:
            return
        self.h[eng].wait_ge(self.sems[key], val)
        self.seen[eng][key] = val
        if key.startswith("dma") and self.lastwait.get(key, 0) < val:
            self.lastwait[key] = val

    def _deps(self, eng, reads, writes):
        for r in reads:
            self._wait(eng, r.w)
        for w in writes:
            self._wait(eng, w.w)
            for k, v in w.r.items():
                self._wait(eng, (k, v))

    def _commit(self, ev, reads, writes):
        k, v = ev
        for r in reads:
            if r.r.get(k, 0) < v:
                r.r[k] = v
        for w in writes:
            w.w = ev
            w.r = {}

    def op(self, eng, reads, writes, fn):
        ex = [r for r in reads if r.excl and r not in writes]
        if ex:
            writes = list(writes) + ex
        self._deps(eng, reads, writes)
        k = self.engkey[eng]
        fn().then_inc(self.sems[k], 1)
        self.cnt[k] += 1
        self._commit((k, self.cnt[k]), reads, writes)

    def dma(self, q, semkey, reads, writes, out, in_, **kw):
        if self.lastwait.get(semkey, 0) > self.serial.get(semkey, 0):
            self._wait(q, (semkey, self.cnt[semkey]))
            self.serial[semkey] = self.cnt[semkey]
        self._deps(q, reads, writes)
        self.h[q].dma_start(out=out, in_=in_, **kw).then_inc(self.sems[semkey], 16)
        self.cnt[semkey] += 16
        self._commit((semkey, self.cnt[semkey]), reads, writes)

    def wait_all(self, eng, ress):
        for r in ress:
            self._wait(eng, r.w)


class Buf:
    def __init__(self, P, name, shape, dt, nres=1):
        self.t = P.sb(name, shape, dt)
        self.res = [Res() for _ in range(nres)]
        self.dsem = None


class Cfg:
    def __init__(s, D=4096, DFF=11008, AH=32, AKV=8, GHK=16, GHV=32, T=384, NT=3, NCORE=8, CK=31):
        s.D, s.DFF, s.AH, s.AKV, s.GHK, s.GHV, s.T, s.NT, s.NCORE, s.CK = D, DFF, AH, AKV, GHK, GHV, T, NT, NCORE, CK
        s.KC = D // 128
        s.FC = DFF // 128
        s.TOK = NT * T
        s.HALO = 128
        s.MAIN = s.TOK - s.HALO
        s.SEQ = NCORE * s.MAIN
        s.NB = T // 128
        s.GKD = GHK * 128
        s.GVD = GHV * 128
        s.GIN = 2 * s.GKD + 2 * s.GVD + 2 * GHV
        s.HVC = GHV // NCORE
        s.HKC = GHK // NCORE
        s.NCH = s.SEQ // 128


V_MIX0, V_FFN0, V_MIX1, V_FFN1, V_MIX2, V_FFN2, V_MIX3, V_FFN3, V_FINAL = range(9)
V_CB1V, V_CB1G, V_CBDW, V_CLNG, V_CLNB, V_CB2 = range(9, 15)
V_DCONV = 15
V_CDW = 18
def NVEC(cfg): return 18 + cfg.CK


class KB:
    def __init__(s, cfg, name, parent=None):
        s.cfg = cfg
        s.parent = parent
        s.fz = parent is not None
        s.over = {}
        if parent is None:
            s.nc = bass.Bass("TRN2", target_bir_lowering=False)
            s.reg = {}
            s.P = None
        else:
            s.nc, s.reg, s.P = parent.nc, parent.reg, parent.P
        s.es = ExitStack()

    def din(s, name, shape, dt=F32):
        if name in s.over:
            return s.over[name]
        if name not in s.reg:
            s.reg[name] = s.nc.dram_tensor(name, list(shape), dt, kind="ExternalInput").ap()
        return s.reg[name]

    def dout(s, name, shape, dt=F32):
        if name in s.over:
            return s.over[name]
        if name not in s.reg:
            s.reg[name] = s.nc.dram_tensor(name, list(shape), dt, kind="ExternalOutput").ap()
        return s.reg[name]

    def setup(s, wb_cols=384, g_bytes=None, need_xh=True):
        cfg, nc = s.cfg, s.nc
        if s.P is None:
            s.P = Prog(nc, s.es)
        P = s.P
        P.es = s.es
        P.prefix = type(s).__name__ + "_" if s.fz else ""
        T, KC = cfg.T, cfg.KC
        if need_xh:
            s.X = Buf(P, "X", [128, KC, T], F32, KC)
            s.H = Buf(P, "H", [128, KC, T], BF16, KC)
            s.NWB = 3
            s.WBC = wb_cols
            s.WB = [Buf(P, "wb%d" % i, [128, 8, wb_cols], BF16) for i in range(s.NWB)]
            for b in s.WB:
                b.dsem = P.new_dma_sem()
            s.wb_i = 0
        if s.fz:
            s.PS, s.PSR = s.parent.PS, s.parent.PSR
        else:
            s.PS = [P.ps("ps%d" % i, [128, 512]) for i in range(8)]
            s.PSR = [Res(excl=True) for _ in range(8)]
        s.bset = 0
        s.misc_i = 0
        cst = s.din("consts", [128, 4, 128])
        s.CF = Buf(P, "cf", [128, 4, 128], F32)
        s.CB = Buf(P, "cb", [128, 4, 128], BF16)
        s.ld = P.new_dma_sem()
        P.dma("sp", s.ld, [], [s.CF.res[0]], s.CF.t[:], cst)
        P.op("dve", [s.CF.res[0]], [s.CB.res[0]], lambda: nc.vector.tensor_copy(out=s.CB.t[:], in_=s.CF.t[:]))
        s.ones_b = s.CB.t[:, 0, :]
        s.ones_f = s.CF.t[:, 0, :]
        s.ident_f = s.CF.t[:, 1, :]
        s.cres = [s.CF.res[0], s.CB.res[0]]
        s.iosem = [P.new_dma_sem() for _ in range(4)]
        if not need_xh:
            return
        s.sq = [Buf(P, "sq%d" % i, [128, T], BF16) for i in range(2)]
        s.sq_i = 0
        s.rstd = Buf(P, "rstd", [128, T], F32)
        s.tmpf = [Buf(P, "tmpf%d" % i, [128, T + 32], F32) for i in range(12)]
        s.tmp_i = 0
        for b in s.tmpf:
            b.dsem = P.new_dma_sem()

    def tmp(s):
        b = s.tmpf[s.tmp_i % len(s.tmpf)]
        s.tmp_i += 1
        return b

    def misc_bank(s):
        i = 6 + (s.misc_i % 2)
        s.misc_i += 1
        return i

    def gemm(s, W, Kc, in_ap, in_res, ocs, evac, Tw, cw=128):
        P, nc = s.P, s.nc
        Wv = W.rearrange("(kc p) n -> p kc n", p=128)
        nbw = s.WBC // 128
        i = 0
        while i < len(ocs):
            blk = [ocs[i]]
            while len(blk) < nbw and i + len(blk) < len(ocs) and ocs[i + len(blk)] == blk[-1] + 1:
                blk.append(ocs[i + len(blk)])
            i += len(blk)
            banks = [s.bset * 3 + j for j in range(len(blk))]
            s.bset ^= 1
            n0 = blk[0] * cw
            ncols = len(blk) * cw
            for kg in range(0, Kc, 8):
                kn = min(8, Kc - kg)
                wb = s.WB[s.wb_i % s.NWB]
                s.wb_i += 1
                P.dma("pool", wb.dsem, [], [wb.res[0]], wb.t[:, 0:kn, 0:ncols], Wv[:, kg:kg + kn, n0:n0 + ncols])
                for kl in range(kn):
                    kc = kg + kl
                    for j, bk in enumerate(banks):
                        P.op("pe", [wb.res[0], in_res(kc)], [s.PSR[bk]],
                             lambda j=j, bk=bk, kl=kl, kc=kc, wb=wb: nc.tensor.matmul(
                                 s.PS[bk][0:cw, 0:Tw], lhsT=wb.t[:, kl, j * cw:(j + 1) * cw], rhs=in_ap(kc),
                                 start=(kc == 0), stop=(kc == Kc - 1)))
            for j, bk in enumerate(banks):
                evac(blk[j], s.PS[bk][0:cw, 0:Tw], s.PSR[bk])

    def rmsnorm(s, gvec, Tw, out_f32=None):
        P, nc, cfg = s.P, s.nc, s.cfg
        KC = cfg.KC
        bk = s.misc_bank()
        for kc in range(KC):
            sq = s.sq[s.sq_i % 2]
            s.sq_i += 1
            P.op("act", [s.X.res[kc]], [sq.res[0]],
                 lambda kc=kc, sq=sq: nc.scalar.activation(out=sq.t[:, 0:Tw], in_=s.X.t[:, kc, 0:Tw], func=AF.Square))
            P.op("pe", [sq.res[0]] + s.cres, [s.PSR[bk]],
                 lambda kc=kc, sq=sq: nc.tensor.matmul(s.PS[bk][:, 0:Tw], lhsT=s.ones_b, rhs=sq.t[:, 0:Tw],
                                                       start=(kc == 0), stop=(kc == KC - 1)))
        P.op("act", [s.PSR[bk]], [s.rstd.res[0]],
             lambda: nc.scalar.activation(out=s.rstd.t[:, 0:Tw], in_=s.PS[bk][:, 0:Tw], func=AF.Sqrt,
                                          scale=1.0 / cfg.D, bias=s.eps6[:, 0:1]))
        P.op("dve", [s.rstd.res[0]], [s.rstd.res[0]],
             lambda: nc.vector.reciprocal(out=s.rstd.t[:, 0:Tw], in_=s.rstd.t[:, 0:Tw]))
        for kc in range(KC):
            if out_f32 is None:
                oap, ores = s.H.t[:, kc, 0:Tw], s.H.res[kc]
            else:
                oap, ores = out_f32(kc)
            P.op("dve", [s.X.res[kc], s.rstd.res[0], s.VEC.res[0]], ores if isinstance(ores, list) else [ores],
                 lambda kc=kc, oap=oap: nc.vector.scalar_tensor_tensor(
                     out=oap, in0=s.X.t[:, kc, 0:Tw], scalar=s.VEC.t[:, gvec, kc:kc + 1],
                     in1=s.rstd.t[:, 0:Tw], op0=ALU.mult, op1=ALU.mult))

    def load_vecs(s, nvec, nfl):
        P, cfg = s.P, s.cfg
        s.VEC = Buf(P, "vec", [128, nvec, cfg.KC], F32)
        P.dma("sp", s.ld, [], [s.VEC.res[0]], s.VEC.t[:], s.din("vecs", [128, nvec, cfg.KC]))
        s.FVEC = Buf(P, "fvec", [128, nfl * 4, cfg.FC], F32)
        P.dma("sp", s.ld, [], [s.FVEC.res[0]], s.FVEC.t[:], s.din("fvecs_%d" % nfl, [128, nfl * 4, cfg.FC]))
        s.HV = Buf(P, "hv", [128, 1], F32)
        P.dma("sp", s.ld, [], [s.HV.res[0]], s.HV.t[:], s.din("hv_in", [128, 1]))
        s.EPS = Buf(P, "eps", [128, 4], F32)
        P.op("dve", [], [s.EPS.res[0]], lambda: s.nc.vector.memset(s.EPS.t[:, 2:3], 1.0))
        s.one1 = s.EPS.t[:, 2:3]
        P.op("dve", [], [s.EPS.res[0]], lambda: s.nc.vector.memset(s.EPS.t[:, 0:1], 1e-6))
        P.op("dve", [], [s.EPS.res[0]], lambda: s.nc.vector.memset(s.EPS.t[:, 1:2], 1e-5))
        s.eps6 = s.EPS.t[:, 0:1]
        s.eps5 = s.EPS.t[:, 1:2]
        s.cres = s.cres + [s.EPS.res[0]]

    def dwconv(s, src, src_res, Kw, wcol, bias_ap, out_ap, out_res, Tw, extra_res=()):
        P, nc = s.P, s.nc
        rd = [src_res] + list(extra_res)
        if not isinstance(out_res, (list, tuple)):
            out_res = [out_res]
        out_res = list(out_res)
        k = Kw - 1
        if bias_ap is not None:
            P.op("dve", rd, out_res, lambda: nc.vector.tensor_scalar(
                out=out_ap, in0=src[:, k:k + Tw], scalar1=wcol(k), scalar2=bias_ap, op0=ALU.mult, op1=ALU.add))
        else:
            P.op("dve", rd, out_res, lambda: nc.vector.tensor_scalar(
                out=out_ap, in0=src[:, k:k + Tw], scalar1=wcol(k), scalar2=None, op0=ALU.mult))
        for k in range(Kw - 2, -1, -1):
            P.op("dve", rd + out_res, out_res, lambda k=k: nc.vector.scalar_tensor_tensor(
                out=out_ap, in0=src[:, k:k + Tw], scalar=wcol(k), in1=out_ap, op0=ALU.mult, op1=ALU.add))

    def ffn(s, li, fl, gvec, Wg, Wu, Wd, ti, Tw):
        P, nc, cfg = s.P, s.nc, s.cfg
        KC, FC = cfg.KC, cfg.FC
        s.rmsnorm(gvec, Tw)
        half = (FC + 1) // 2
        car = s.fcar[li]
        for h0 in range(0, FC, half):
            fcs = list(range(h0, min(FC, h0 + half)))
            gb = {}

            def evac_gate(fc, ps, psr):
                b = s.tmp()
                gb[fc] = b
                if ti == 0:
                    P.op("dve", [], [b.res[0]], lambda: nc.vector.memset(b.t[:, 0:2], 0.0))
                else:
                    P.op("dve", [car.res[fc]], [b.res[0]],
                         lambda: nc.vector.tensor_copy(out=b.t[:, 0:2], in_=car.t[:, fc, :]))
                P.op("act", [psr], [b.res[0]], lambda: nc.scalar.copy(out=b.t[:, 2:2 + Tw], in_=ps))
                if ti == 0:
                    P.op("dve", [b.res[0], s.HV.res[0]], [b.res[0]], lambda: nc.vector.tensor_scalar(
                        out=b.t[:, 2:2 + 128], in0=b.t[:, 2:2 + 128], scalar1=s.HV.t[:, 0:1], scalar2=None, op0=ALU.mult))
                P.op("dve", [b.res[0]], [car.res[fc]],
                     lambda: nc.vector.tensor_copy(out=car.t[:, fc, :], in_=b.t[:, Tw:Tw + 2]))

            def evac_up(fc, ps, psr):
                b = gb.pop(fc)
                c = s.tmp()
                s.dwconv(b.t, b.res[0], 3, lambda k: s.FVEC.t[:, fl * 4 + k, fc:fc + 1],
                         s.FVEC.t[:, fl * 4 + 3, fc:fc + 1], c.t[:, 0:Tw], c.res[0], Tw, [s.FVEC.res[0]])
                P.op("act", [c.res[0]], [c.res[0]],
                     lambda: nc.scalar.activation(out=c.t[:, 0:Tw], in_=c.t[:, 0:Tw], func=AF.Silu))
                P.op("dve", [c.res[0], psr], [s.G.res[fc - h0]], lambda: nc.vector.tensor_tensor(
                    out=s.Gb[:, fc - h0, 0:Tw], in0=c.t[:, 0:Tw], in1=ps, op=ALU.mult))

            nbw = s.WBC // 128
            for i in range(0, len(fcs), nbw):
                blk = fcs[i:i + nbw]
                s.gemm(Wg, KC, lambda kc: s.H.t[:, kc, 0:Tw], lambda kc: s.H.res[kc], blk, evac_gate, Tw)
                s.gemm(Wu, KC, lambda kc: s.H.t[:, kc, 0:Tw], lambda kc: s.H.res[kc], blk, evac_up, Tw)

            def evac_down(oc, ps, psr):
                P.op("dve", [s.X.res[oc], psr], [s.X.res[oc]], lambda: nc.vector.tensor_tensor(
                    out=s.X.t[:, oc, 0:Tw], in0=s.X.t[:, oc, 0:Tw], in1=ps, op=ALU.add))

            Wd_h = Wd[h0 * 128:(h0 + len(fcs)) * 128, :]
            s.gemm(Wd_h, len(fcs), lambda kc: s.Gb[:, kc, 0:Tw], lambda kc: s.G.res[kc], list(range(KC)), evac_down, Tw)

    def alloc_g(s):
        P, cfg = s.P, s.cfg
        T, KC = cfg.T, cfg.KC
        s.G = Buf(P, "G", [128, KC * T], F32, 2 * KC)
        s.Gf = s.G.t[:].rearrange("p (n t) -> p n t", t=T)
        s.Gb = s.G.t[:].bitcast(BF16).rearrange("p (n t) -> p n t", t=T)

    def gfres(s, k):
        return [s.G.res[2 * k], s.G.res[2 * k + 1]]

    def finish(s, out_res):
        s.P.wait_all("sp", out_res)


class StageC(KB):
    def build(s):
        cfg, nc = s.cfg, s.nc
        D, KC, T, TOK, MAIN = cfg.D, cfg.KC, cfg.T, cfg.TOK, cfg.MAIN
        s.setup()
        P = s.P
        s.alloc_g()
        s.load_vecs(NVEC(cfg), 3)
        x2T = s.din("x2T", [D, TOK])
        zsT = s.din("zsT", [D, TOK])
        if not s.fz:
            onT = s.din("onT", [D, TOK])
            onv = onT.rearrange("(kc p) t -> p kc t", p=128)
        else:
            cid = s.parent.cid
            NCR, HVC = cfg.NCORE, cfg.HVC
            bigv = s.over["big"].rearrange("b (r q) c -> b r q c", r=NCR)
            mybig = s.over["mybig"]
            myres = Res()
            P.dma("sp", s.iosem[1], [], [myres], mybig[:, 0:128, :].rearrange("r q c -> r (q c)"),
                  bigv[bass.ds(cid, 1), :, cfg.MAIN - 128:cfg.MAIN, :].rearrange("o r q c -> (o r) (q c)"))
            P.dma("sp", s.iosem[2], [], [myres], mybig[:, 128:cfg.TOK, :].rearrange("r q c -> r (q c)"),
                  bigv[bass.ds(cid + 1, 1), :, :, :].rearrange("o r q c -> (o r) (q c)"))
        W = {}
        for nm, shp in [("b_w_o", [D, D]), ("c_w_pw1", [D, 2 * D]), ("c_w_pw2", [D, D]), ("d_w_in", [D, 3 * D]),
                        ("d_w_out", [D, D])]:
            W[nm] = s.din(nm, shp)
        for l in (1, 2, 3):
            W["wg%d" % l] = s.din("wg%d" % l, [D, cfg.DFF])
            W["wu%d" % l] = s.din("wu%d" % l, [D, cfg.DFF])
            W["wd%d" % l] = s.din("wd%d" % l, [cfg.DFF, D])
        outT = s.dout("outT", [D, MAIN])
        s.fcar = [Buf(P, "fcar%d" % i, [128, cfg.FC, 2], F32, cfg.FC) for i in range(3)]
        s.ccar = Buf(P, "ccar", [128, KC, cfg.CK - 1], F32, KC)
        s.dcar = Buf(P, "dcar", [128, KC, 2], F32, KC)
        s.ubuf = [Buf(P, "ubuf%d" % i, [128, cfg.CK - 1 + T], F32) for i in range(2)]
        s.u_i = 0
        s.stat = Buf(P, "stat", [128, 2, T], F32, 2)
        x2v = x2T.rearrange("(kc p) t -> p kc t", p=128)
        zsv = zsT.rearrange("(kc p) t -> p kc t", p=128)
        outv = outT.rearrange("(kc p) t -> p kc t", p=128)
        out_res = []
        for ti in range(cfg.NT):
            t0 = ti * T
            Tw = T
            for kc in range(0, KC, 8):
                ke = min(KC, kc + 8)
                P.dma("sp", s.iosem[0], [], s.X.res[kc:ke], s.X.t[:, kc:ke, :], x2v[:, kc:ke, t0:t0 + T])
            for kc in range(KC):
                a, b = s.tmp(), s.tmp()
                P.dma("sp", b.dsem, [], [b.res[0]], b.t[:, 0:T], zsv[:, kc, t0:t0 + T])
                if not s.fz:
                    P.dma("sp", a.dsem, [], [a.res[0]], a.t[:, 0:T], onv[:, kc, t0:t0 + T])
                    P.op("dve", [a.res[0], b.res[0]], [s.H.res[kc]], lambda a=a, b=b, kc=kc: nc.vector.tensor_tensor(
                        out=s.H.t[:, kc, :], in0=a.t[:, 0:T], in1=b.t[:, 0:T], op=ALU.mult))
                    continue
                r, hv = kc // HVC, kc % HVC
                mb = s.misc_bank()
                for bb in range(cfg.NB):
                    bl = ti * cfg.NB + bb
                    src = mybig[r, bl * 128:(bl + 1) * 128, hv * 128:(hv + 1) * 128]
                    P.dma("sp", a.dsem, [myres], [a.res[0]], a.t[:, bb * 128:(bb + 1) * 128], src)
                for bb in range(cfg.NB):
                    P.op("pe", [a.res[0]] + s.cres, [s.PSR[mb]], lambda bb=bb: nc.tensor.transpose(
                        out=s.PS[mb][:, bb * 128:(bb + 1) * 128], in_=a.t[:, bb * 128:(bb + 1) * 128], identity=s.ident_f))
                P.op("dve", [s.PSR[mb], b.res[0]], [s.H.res[kc]], lambda b=b, kc=kc: nc.vector.tensor_tensor(
                    out=s.H.t[:, kc, :], in0=b.t[:, 0:T], in1=s.PS[mb][:, 0:T], op=ALU.mult))
            s.gemm(W["b_w_o"], KC, lambda kc: s.H.t[:, kc, 0:Tw], lambda kc: s.H.res[kc], list(range(KC)), s.evac_resid(Tw), Tw)
            s.ffn(0, 0, V_FFN1, W["wg1"], W["wu1"], W["wd1"], ti, Tw)
            s.conformer(W["c_w_pw1"], W["c_w_pw2"], ti, Tw)
            s.ffn(1, 1, V_FFN2, W["wg2"], W["wu2"], W["wd2"], ti, Tw)
            s.shortconv(W["d_w_in"], W["d_w_out"], ti, Tw)
            s.ffn(2, 2, V_FFN3, W["wg3"], W["wu3"], W["wd3"], ti, Tw)
            s.rmsnorm(V_FINAL, Tw, out_f32=lambda kc: (s.Gf[:, kc, 0:Tw], s.gfres(kc)))
            c0 = cfg.HALO if ti == 0 else 0
            o0 = t0 + c0 - cfg.HALO
            for kc in range(0, KC, 8):
                ke = min(KC, kc + 8)
                rr = [r for k in range(kc, ke) for r in s.gfres(k)]
                dres = Res()
                P.dma("sp", s.iosem[3], rr, [dres], outv[:, kc:ke, o0:o0 + T - c0], s.Gf[:, kc:ke, c0:T])
                out_res.append(dres)
        s.finish(out_res)
        return s

    def evac_resid(s, Tw):
        P, nc = s.P, s.nc

        def ev(oc, ps, psr):
            P.op("dve", [s.X.res[oc], psr], [s.X.res[oc]], lambda: nc.vector.tensor_tensor(
                out=s.X.t[:, oc, 0:Tw], in0=s.X.t[:, oc, 0:Tw], in1=ps, op=ALU.add))
        return ev

    def conformer(s, W1, W2, ti, Tw):
        P, nc, cfg = s.P, s.nc, s.cfg
        KC, D, CK = cfg.KC, cfg.D, cfg.CK
        HK = CK - 1
        s.rmsnorm(V_MIX2, Tw)
        sg = {}

        def evac_gate(oc, ps, psr):
            c = oc - KC
            b = s.tmp()
            sg[c] = b
            P.op("act", [psr, s.VEC.res[0]], [b.res[0]], lambda: nc.scalar.activation(
                out=b.t[:, 0:Tw], in_=ps, func=AF.Sigmoid, bias=s.VEC.t[:, V_CB1G, c:c + 1], scale=1.0))

        def evac_val(c, ps, psr):
            b = sg.pop(c)
            u = s.ubuf[s.u_i % 2]
            s.u_i += 1
            if ti == 0:
                P.op("dve", [], [u.res[0]], lambda: nc.vector.memset(u.t[:, 0:HK], 0.0))
            else:
                P.op("dve", [s.ccar.res[c]], [u.res[0]], lambda: nc.vector.tensor_copy(out=u.t[:, 0:HK], in_=s.ccar.t[:, c, :]))
            P.op("dve", [psr, b.res[0], s.VEC.res[0]], [u.res[0]], lambda: nc.vector.scalar_tensor_tensor(
                out=u.t[:, HK:HK + Tw], in0=ps, scalar=s.VEC.t[:, V_CB1V, c:c + 1], in1=b.t[:, 0:Tw],
                op0=ALU.add, op1=ALU.mult))
            if ti == 0:
                P.op("dve", [u.res[0], s.HV.res[0]], [u.res[0]], lambda: nc.vector.tensor_scalar(
                    out=u.t[:, HK:HK + 128], in0=u.t[:, HK:HK + 128], scalar1=s.HV.t[:, 0:1], scalar2=None, op0=ALU.mult))
            P.op("dve", [u.res[0]], [s.ccar.res[c]], lambda: nc.vector.tensor_copy(out=s.ccar.t[:, c, :], in_=u.t[:, Tw:Tw + HK]))
            s.dwconv(u.t, u.res[0], CK, lambda k: s.VEC.t[:, V_CDW + k, c:c + 1], s.VEC.t[:, V_CBDW, c:c + 1],
                     s.Gf[:, c, 0:Tw], s.gfres(c), Tw, [s.VEC.res[0]])

        nbw = s.WBC // 128
        for i in range(0, KC, nbw):
            blk = list(range(i, min(KC, i + nbw)))
            s.gemm(W1, KC, lambda kc: s.H.t[:, kc, 0:Tw], lambda kc: s.H.res[kc], [KC + c for c in blk], evac_gate, Tw)
            s.gemm(W1, KC, lambda kc: s.H.t[:, kc, 0:Tw], lambda kc: s.H.res[kc], blk, evac_val, Tw)
        b1, b2 = s.misc_bank(), s.misc_bank()
        for c in range(KC):
            P.op("pe", s.gfres(c) + s.cres, [s.PSR[b1]], lambda c=c: nc.tensor.matmul(
                s.PS[b1][:, 0:Tw], lhsT=s.ones_f, rhs=s.Gf[:, c, 0:Tw], start=(c == 0), stop=(c == KC - 1)))
        for c in range(KC):
            q = s.tmp()
            P.op("act", s.gfres(c), [q.res[0]], lambda c=c, q=q: nc.scalar.activation(
                out=q.t[:, 0:Tw], in_=s.Gf[:, c, 0:Tw], func=AF.Square))
            P.op("pe", [q.res[0]] + s.cres, [s.PSR[b2]], lambda c=c, q=q: nc.tensor.matmul(
                s.PS[b2][:, 0:Tw], lhsT=s.ones_f, rhs=q.t[:, 0:Tw], start=(c == 0), stop=(c == KC - 1)))
        mean, rs = s.stat.t[:, 0, 0:Tw], s.stat.t[:, 1, 0:Tw]
        sr = s.stat.res
        P.op("act", [s.PSR[b1]], [sr[0]], lambda: nc.scalar.activation(out=mean, in_=s.PS[b1][:, 0:Tw], func=AF.Copy, scale=1.0 / D))
        q = s.tmp()
        P.op("dve", [sr[0]], [q.res[0]], lambda: nc.vector.tensor_tensor(out=q.t[:, 0:Tw], in0=mean, in1=mean, op=ALU.mult))
        P.op("dve", [s.PSR[b2], q.res[0]], [sr[1]], lambda: nc.vector.scalar_tensor_tensor(
            out=rs, in0=s.PS[b2][:, 0:Tw], scalar=1.0 / D, in1=q.t[:, 0:Tw], op0=ALU.mult, op1=ALU.subtract))
        P.op("act", [sr[1]] + s.cres, [sr[1]], lambda: nc.scalar.activation(out=rs, in_=rs, func=AF.Sqrt, bias=s.eps5, scale=1.0))
        P.op("dve", [sr[1]], [sr[1]], lambda: nc.vector.reciprocal(out=rs, in_=rs))
        for c in range(KC):
            q = s.tmp()
            P.op("dve", s.gfres(c) + [sr[0]], [q.res[0]], lambda c=c, q=q: nc.vector.tensor_tensor(
                out=q.t[:, 0:Tw], in0=s.Gf[:, c, 0:Tw], in1=mean, op=ALU.subtract))
            P.op("dve", [q.res[0], sr[1]], [q.res[0]], lambda q=q: nc.vector.tensor_tensor(
                out=q.t[:, 0:Tw], in0=q.t[:, 0:Tw], in1=rs, op=ALU.mult))
            P.op("act", [q.res[0], s.VEC.res[0]], [s.H.res[c]], lambda c=c, q=q: nc.scalar.activation(
                out=s.H.t[:, c, 0:Tw], in_=q.t[:, 0:Tw], func=AF.Silu, scale=s.VEC.t[:, V_CLNG, c:c + 1],
                bias=s.VEC.t[:, V_CLNB, c:c + 1]))

        def evac2(oc, ps, psr):
            P.op("dve", [s.X.res[oc], psr, s.VEC.res[0]], [s.X.res[oc]], lambda: nc.vector.scalar_tensor_tensor(
                out=s.X.t[:, oc, 0:Tw], in0=ps, scalar=s.VEC.t[:, V_CB2, oc:oc + 1], in1=s.X.t[:, oc, 0:Tw],
                op0=ALU.add, op1=ALU.add))
        s.gemm(W2, KC, lambda kc: s.H.t[:, kc, 0:Tw], lambda kc: s.H.res[kc], list(range(KC)), evac2, Tw)

    def shortconv(s, Win, Wout, ti, Tw):
        P, nc, cfg = s.P, s.nc, s.cfg
        KC = cfg.KC
        s.rmsnorm(V_MIX3, Tw)
        cgb, cvb = {}, {}
        hin = (lambda kc: s.H.t[:, kc, 0:Tw]), (lambda kc: s.H.res[kc])

        def evac_cg(oc, ps, psr):
            b = s.tmp()
            cgb[oc - KC] = b
            P.op("act", [psr], [b.res[0]], lambda: nc.scalar.copy(out=b.t[:, 0:Tw], in_=ps))

        def evac_xin(oc, ps, psr):
            c = oc - 2 * KC
            b = cgb.pop(c)
            u = s.tmp()
            if ti == 0:
                P.op("dve", [], [u.res[0]], lambda: nc.vector.memset(u.t[:, 0:2], 0.0))
            else:
                P.op("dve", [s.dcar.res[c]], [u.res[0]], lambda: nc.vector.tensor_copy(out=u.t[:, 0:2], in_=s.dcar.t[:, c, :]))
            P.op("dve", [psr, b.res[0]], [u.res[0]], lambda: nc.vector.tensor_tensor(
                out=u.t[:, 2:2 + Tw], in0=b.t[:, 0:Tw], in1=ps, op=ALU.mult))
            if ti == 0:
                P.op("dve", [u.res[0], s.HV.res[0]], [u.res[0]], lambda: nc.vector.tensor_scalar(
                    out=u.t[:, 2:2 + 128], in0=u.t[:, 2:2 + 128], scalar1=s.HV.t[:, 0:1], scalar2=None, op0=ALU.mult))
            P.op("dve", [u.res[0]], [s.dcar.res[c]], lambda: nc.vector.tensor_copy(out=s.dcar.t[:, c, :], in_=u.t[:, Tw:Tw + 2]))
            v = s.tmp()
            cvb[c] = v
            s.dwconv(u.t, u.res[0], 3, lambda k: s.VEC.t[:, V_DCONV + k, c:c + 1], None, v.t[:, 0:Tw], v.res[0], Tw, [s.VEC.res[0]])

        def evac_bg(c, ps, psr):
            v = cvb.pop(c)
            P.op("dve", [psr, v.res[0]], [s.G.res[c]], lambda: nc.vector.tensor_tensor(
                out=s.Gb[:, c, 0:Tw], in0=v.t[:, 0:Tw], in1=ps, op=ALU.mult))

        nbw = s.WBC // 128
        for i in range(0, KC, nbw):
            blk = list(range(i, min(KC, i + nbw)))
            s.gemm(Win, KC, hin[0], hin[1], [KC + c for c in blk], evac_cg, Tw)
            s.gemm(Win, KC, hin[0], hin[1], [2 * KC + c for c in blk], evac_xin, Tw)
            s.gemm(Win, KC, hin[0], hin[1], blk, evac_bg, Tw)
        s.gemm(Wout, KC, lambda kc: s.Gb[:, kc, 0:Tw], lambda kc: s.G.res[kc], list(range(KC)), s.evac_resid(Tw), Tw)


class StageA(KB):
    def build(s):
        cfg, nc = s.cfg, s.nc
        D, KC, T, TOK, MAIN, NB = cfg.D, cfg.KC, cfg.T, cfg.TOK, cfg.MAIN, cfg.NB
        AH, AKV = cfg.AH, cfg.AKV
        s.setup()
        P = s.P
        s.alloc_g()
        s.load_vecs(NVEC(cfg), 1)
        xT = s.din("xT", [D, 128 + TOK])
        Wqkv = s.din("a_w_qkv", [D, (AH + 2 * AKV) * 128])
        Wo = s.din("a_w_o", [D, D])
        Wg, Wu, Wd = s.din("wg0", [D, cfg.DFF]), s.din("wu0", [D, cfg.DFF]), s.din("wd0", [cfg.DFF, D])
        Win = s.din("b_w_in", [D, cfg.GIN])
        if s.fz:
            x2T, zsT = s.over["x2s"], s.over["zss"]
            qkT = vT = gbT = None
        else:
            x2T = s.dout("x2T", [D, MAIN])
            qkT = s.dout("qkT", [2 * cfg.GKD, MAIN])
            vT = s.dout("vT", [cfg.GVD, MAIN])
            zsT = s.dout("zsT", [cfg.GVD, MAIN])
            gbT = s.dout("gbT", [2 * cfg.GHV, MAIN])
        s.fcar = [Buf(P, "fcar0", [128, cfg.FC, 2], F32, cfg.FC)]
        s.TAB = Buf(P, "tab", [128, 2, 128 + TOK], F32)
        P.dma("sp", s.ld, [], [s.TAB.res[0]], s.TAB.t[:], s.din("rope_tab", [128, 2, 128 + TOK]))
        s.MK = Buf(P, "mk", [128, 2, 256], F32)
        P.dma("sp", s.ld, [], [s.MK.res[0]], s.MK.t[:], s.din("masks", [128, 2, 256]))
        s.SK = Buf(P, "sk", [128, AH], F32)
        P.dma("sp", s.ld, [], [s.SK.res[0]], s.SK.t[:], s.din("sinkb", [128, AH]))
        NQK = 2 * cfg.GHK + cfg.GHV
        s.GCV = Buf(P, "gcv", [128, 4, NQK], F32)
        P.dma("sp", s.ld, [], [s.GCV.res[0]], s.GCV.t[:], s.din("gconv", [128, 4, NQK]))
        s.GS = Buf(P, "gs", [cfg.GHV, 2], F32)
        P.dma("sp", s.ld, [], [s.GS.res[0]], s.GS.t[:], s.din("gsc", [cfg.GHV, 2]))
        P.op("act", [s.GS.res[0]], [s.GS.res[0]], lambda: nc.scalar.activation(out=s.GS.t[:, 1:2], in_=s.GS.t[:, 1:2], func=AF.Exp))
        P.op("dve", [s.GS.res[0]], [s.GS.res[0]], lambda: nc.vector.tensor_scalar(
            out=s.GS.t[:, 1:2], in0=s.GS.t[:, 1:2], scalar1=-1.0, scalar2=None, op0=ALU.mult))
        s.KT = Buf(P, "KT", [128, AKV, 128 + T], BF16, AKV)
        s.V = Buf(P, "V", [128, 1 + NB, AKV * 128], BF16, 1 + NB)
        s.gcar = Buf(P, "gcar", [128, NQK, 3], F32, NQK)
        s.st = [Buf(P, "st%d" % i, [128, 8], F32) for i in range(4)]
        s.pts = [Buf(P, "pts%d" % i, [128, 256], BF16) for i in range(2)]
        s.u_i = 0
        xv = xT.rearrange("(kc p) t -> p kc t", p=128)
        x2v = x2T.rearrange("(kc p) t -> p kc t", p=128)
        out_res = []
        hin = (lambda Tw: (lambda kc: s.H.t[:, kc, 0:Tw])), (lambda kc: s.H.res[kc])

        def load_x(c0, Tw):
            for kc in range(0, KC, 8):
                ke = min(KC, kc + 8)
                P.dma("sp", s.iosem[0], [], s.X.res[kc:ke], s.X.t[:, kc:ke, 0:Tw], xv[:, kc:ke, c0:c0 + Tw])

        def rope_evac(dst, tcol0, Tw):
            def ev(oc, ps, psr):
                ap, res = dst(oc)
                qf = s.tmp()
                P.op("act", [psr], [qf.res[0]], lambda: nc.scalar.copy(out=qf.t[:, 0:Tw], in_=ps))
                mb = s.misc_bank()
                P.op("pe", [qf.res[0]] + s.cres, [s.PSR[mb]], lambda: nc.tensor.matmul(
                    s.PS[mb][:, 0:Tw], lhsT=s.CF.t[:, 2, :], rhs=qf.t[:, 0:Tw], start=True, stop=True))
                t1, t2 = s.tmp(), s.tmp()
                P.op("dve", [qf.res[0], s.TAB.res[0]], [t1.res[0]], lambda: nc.vector.tensor_tensor(
                    out=t1.t[:, 0:Tw], in0=qf.t[:, 0:Tw], in1=s.TAB.t[:, 0, tcol0:tcol0 + Tw], op=ALU.mult))
                P.op("dve", [s.PSR[mb], s.TAB.res[0]], [t2.res[0]], lambda: nc.vector.tensor_tensor(
                    out=t2.t[:, 0:Tw], in0=s.PS[mb][:, 0:Tw], in1=s.TAB.t[:, 1, tcol0:tcol0 + Tw], op=ALU.mult))
                P.op("dve", [t1.res[0], t2.res[0]], [res], lambda: nc.vector.tensor_tensor(
                    out=ap, in0=t1.t[:, 0:Tw], in1=t2.t[:, 0:Tw], op=ALU.add))
            return ev

        def vgemm(nblk, vb0):
            Wv = Wqkv.rearrange("(kc p) n -> p kc n", p=128)
            vc0 = (AH + AKV) * 128
            ntot = AKV * 128
            n0 = 0
            while n0 < ntot:
                ncols = min(s.WBC, ntot - n0)
                banks = [s.bset * 3 + j for j in range(nblk)]
                s.bset ^= 1
                for kg in range(0, KC, 8):
                    kn = min(8, KC - kg)
                    wb = s.WB[s.wb_i % s.NWB]
                    s.wb_i += 1
                    P.dma("pool", wb.dsem, [], [wb.res[0]], wb.t[:, 0:kn, 0:ncols], Wv[:, kg:kg + kn, vc0 + n0:vc0 + n0 + ncols])
                    for kl in range(kn):
                        kc = kg + kl
                        for b, bk in enumerate(banks):
                            P.op("pe", [wb.res[0], s.H.res[kc]], [s.PSR[bk]], lambda b=b, bk=bk, kl=kl, kc=kc, wb=wb: nc.tensor.matmul(
                                s.PS[bk][:, 0:ncols], lhsT=s.H.t[:, kc, b * 128:(b + 1) * 128], rhs=wb.t[:, kl, 0:ncols],
                                start=(kc == 0), stop=(kc == KC - 1)))
                for b, bk in enumerate(banks):
                    P.op("act", [s.PSR[bk]], [s.V.res[vb0 + b]], lambda b=b, bk=bk: nc.scalar.copy(
                        out=s.V.t[:, vb0 + b, n0:n0 + ncols], in_=s.PS[bk][:, 0:ncols]))
                n0 += ncols

        load_x(0, 128)
        s.rmsnorm(V_MIX0, 128)
        s.gemm(Wqkv, KC, hin[0](128), hin[1], [AH + g for g in range(AKV)],
               rope_evac(lambda oc: (s.KT.t[:, oc - AH, 0:128], s.KT.res[oc - AH]), 0, 128), 128)
        vgemm(1, 0)

        for ti in range(cfg.NT):
            t0 = 128 + ti * T
            Tw = T
            load_x(t0, T)
            s.rmsnorm(V_MIX0, Tw)
            if s.fz and ti > 0:
                s.parent.gather_tile(s, ti - 1)
            s.gemm(Wqkv, KC, hin[0](Tw), hin[1], list(range(AH)),
                   rope_evac(lambda oc: (s.Gb[:, oc, 0:Tw], s.G.res[oc]), t0, Tw), Tw)
            s.gemm(Wqkv, KC, hin[0](Tw), hin[1], [AH + g for g in range(AKV)],
                   rope_evac(lambda oc: (s.KT.t[:, oc - AH, 128:128 + Tw], s.KT.res[oc - AH]), t0, Tw), Tw)
            vgemm(NB, 1)
            u = 0
            for b in range(NB):
                for h in range(AH):
                    s.attn_unit(ti, b, h, u)
                    u += 1
            for g in range(AKV):
                P.op("dve", [s.KT.res[g]], [s.KT.res[g]], lambda g=g: nc.vector.tensor_copy(
                    out=s.KT.t[:, g, 0:128], in_=s.KT.t[:, g, T:T + 128]))
            P.op("dve", [s.V.res[NB]], [s.V.res[0]], lambda: nc.vector.tensor_copy(out=s.V.t[:, 0, :], in_=s.V.t[:, NB, :]))
            s.gemm(Wo, KC, hin[0](Tw), hin[1], list(range(KC)), s.evac_resid(Tw), Tw)
            s.ffn(0, 0, V_FFN0, Wg, Wu, Wd, ti, Tw)
            c0 = cfg.HALO if ti == 0 else 0
            o0 = ti * T + c0 - cfg.HALO
            nm = T - c0
            if s.fz:
                c0, o0, nm = 0, ti * T, T
            for kc in range(0, KC, 8):
                ke = min(KC, kc + 8)
                dres = Res()
                P.dma("sp", s.iosem[3], s.X.res[kc:ke], [dres], x2v[:, kc:ke, o0:o0 + nm], s.X.t[:, kc:ke, c0:T])
                out_res.append(dres)
            s.rmsnorm(V_MIX1, Tw)
            s.gdn_pre(Win, qkT, vT, zsT, gbT, ti, Tw, c0, o0, nm, out_res)
        s.finish(out_res)
        return s

    def evac_resid(s, Tw):
        return StageC.evac_resid(s, Tw)

    def attn_unit(s, ti, b, h, u):
        P, nc, cfg = s.P, s.nc, s.cfg
        g = h // (cfg.AH // cfg.AKV)
        sbk = s.misc_bank()
        P.op("pe", [s.G.res[h], s.KT.res[g]], [s.PSR[sbk]], lambda: nc.tensor.matmul(
            s.PS[sbk][:, 0:256], lhsT=s.Gb[:, h, b * 128:(b + 1) * 128], rhs=s.KT.t[:, g, b * 128:b * 128 + 256],
            start=True, stop=True))
        sm = s.tmp()
        st = s.st[u % 4]
        mi = 1 if (ti == 0 and b == 1) else 0
        smr, str_ = sm.res[0], st.res[0]
        P.op("dve", [s.PSR[sbk], s.MK.res[0]], [smr], lambda: nc.vector.scalar_tensor_tensor(
            out=sm.t[:, 0:256], in0=s.PS[sbk][:, 0:256], scalar=128.0 ** -0.5, in1=s.MK.t[:, mi, :], op0=ALU.mult, op1=ALU.add))
        P.op("dve", [smr], [str_], lambda: nc.vector.reduce_max(out=st.t[:, 0:1], in_=sm.t[:, 0:256], axis=AX.X))
        P.op("dve", [str_, s.SK.res[0]], [str_], lambda: nc.vector.tensor_tensor(
            out=st.t[:, 1:2], in0=st.t[:, 0:1], in1=s.SK.t[:, h:h + 1], op=ALU.max))
        P.op("dve", [str_], [str_], lambda: nc.vector.tensor_scalar(
            out=st.t[:, 2:3], in0=st.t[:, 1:2], scalar1=-1.0, scalar2=None, op0=ALU.mult))
        P.op("act", [smr, str_], [smr, str_], lambda: nc.scalar.activation(
            out=sm.t[:, 0:256], in_=sm.t[:, 0:256], func=AF.Exp, bias=st.t[:, 2:3], scale=1.0, accum_out=st.t[:, 3:4]))
        P.op("act", [str_, s.SK.res[0]], [str_], lambda: nc.scalar.activation(
            out=st.t[:, 4:5], in_=s.SK.t[:, h:h + 1], func=AF.Exp, bias=st.t[:, 2:3], scale=1.0))
        P.op("dve", [str_], [str_], lambda: nc.vector.tensor_tensor(
            out=st.t[:, 5:6], in0=st.t[:, 3:4], in1=st.t[:, 4:5], op=ALU.add))
        P.op("dve", [str_], [str_], lambda: nc.vector.reciprocal(out=st.t[:, 6:7], in_=st.t[:, 5:6]))
        P.op("dve", [smr, str_], [smr], lambda: nc.vector.tensor_scalar(
            out=sm.t[:, 0:256], in0=sm.t[:, 0:256], scalar1=st.t[:, 6:7], scalar2=None, op0=ALU.mult))
        pb = u % 3
        ob = 3 + (u % 3)
        for j in range(2):
            P.op("pe", [smr] + s.cres, [s.PSR[pb]], lambda j=j: nc.tensor.transpose(
                out=s.PS[pb][:, j * 128:(j + 1) * 128], in_=sm.t[:, j * 128:(j + 1) * 128], identity=s.ident_f))
        pts = s.pts[u % 2]
        P.op("act", [s.PSR[pb]], [pts.res[0]], lambda: nc.scalar.copy(out=pts.t[:, :], in_=s.PS[pb][:, 0:256]))
        for j in range(2):
            P.op("pe", [pts.res[0], s.V.res[b + j]], [s.PSR[ob]], lambda j=j: nc.tensor.matmul(
                s.PS[ob][:, 0:128], lhsT=s.V.t[:, b + j, g * 128:(g + 1) * 128], rhs=pts.t[:, j * 128:(j + 1) * 128],
                start=(j == 0), stop=(j == 1)))
        P.op("act", [s.PSR[ob]], [s.H.res[h]], lambda: nc.scalar.copy(
            out=s.H.t[:, h, b * 128:(b + 1) * 128], in_=s.PS[ob][:, 0:128]))

    def gdn_pre(s, Win, qkT, vT, zsT, gbT, ti, Tw, c0, o0, nm, out_res):
        P, nc, cfg = s.P, s.nc, s.cfg
        KC = cfg.KC
        NQ2 = 2 * cfg.GHK
        NQK = NQ2 + cfg.GHV
        hin = (lambda kc: s.H.t[:, kc, 0:Tw]), (lambda kc: s.H.res[kc])
        zv_ = zsT.rearrange("(c p) t -> p c t", p=128)
        GHK_, GHV_ = cfg.GHK, cfg.GHV
        if s.fz:
            gin = s.over["gin"][ti]
            HVh = cfg.HVC // 2

            def qk_dst(oc):
                g_, i_ = (0, oc) if oc < GHK_ else (1, oc - GHK_)
                return gin[g_][i_ * 128:(i_ + 1) * 128, :]

            def v_dst(j):
                c2, hv = j // cfg.HVC, j % cfg.HVC
                r0 = (c2 * HVh + hv % HVh) * 128
                return gin[2 + hv // HVh][r0:r0 + 128, :]
            b_dst = gin[4][0:GHV_, :]
            a_dst = gin[4][GHV_:2 * GHV_, :]
        else:
            qkv_ = qkT.rearrange("(c p) t -> p c t", p=128)
            vv_ = vT.rearrange("(c p) t -> p c t", p=128)
            qk_dst = lambda oc: qkv_[:, oc, o0:o0 + nm]
            v_dst = lambda j: vv_[:, j, o0:o0 + nm]
            b_dst = gbT[0:GHV_, o0:o0 + nm]
            a_dst = gbT[GHV_:2 * GHV_, o0:o0 + nm]

        if s.fz and not hasattr(s, "grp_res"):
            s.grp_res = [[[] for _ in range(5)] for _ in range(cfg.NT)]

        def store(src, dst_ap, grp=None):
            dres = Res()
            P.dma("sp", src.dsem, [src.res[0]], [dres], dst_ap, src.t[:, c0:c0 + nm])
            out_res.append(dres)
            if s.fz and grp is not None:
                s.grp_res[ti][grp].append(dres)

        def evac_qkv(oc, ps, psr):
            cb = s.tmp()
            if ti == 0:
                P.op("dve", [], [cb.res[0]], lambda: nc.vector.memset(cb.t[:, 0:3], 0.0))
            else:
                P.op("dve", [s.gcar.res[oc]], [cb.res[0]], lambda: nc.vector.tensor_copy(out=cb.t[:, 0:3], in_=s.gcar.t[:, oc, :]))
            P.op("act", [psr], [cb.res[0]], lambda: nc.scalar.copy(out=cb.t[:, 3:3 + Tw], in_=ps))
            if ti == 0:
                P.op("dve", [cb.res[0], s.HV.res[0]], [cb.res[0]], lambda: nc.vector.tensor_scalar(
                    out=cb.t[:, 3:3 + 128], in0=cb.t[:, 3:3 + 128], scalar1=s.HV.t[:, 0:1], scalar2=None, op0=ALU.mult))
            P.op("dve", [cb.res[0]], [s.gcar.res[oc]], lambda: nc.vector.tensor_copy(out=s.gcar.t[:, oc, :], in_=cb.t[:, Tw:Tw + 3]))
            y = s.tmp()
            s.dwconv(cb.t, cb.res[0], 4, lambda k: s.GCV.t[:, k, oc:oc + 1], None, y.t[:, 0:Tw], y.res[0], Tw, [s.GCV.res[0]])
            P.op("act", [y.res[0]], [y.res[0]], lambda: nc.scalar.activation(out=y.t[:, 0:Tw], in_=y.t[:, 0:Tw], func=AF.Silu))
            if oc >= NQ2:
                store(y, v_dst(oc - NQ2), 2 + ((oc - NQ2) % cfg.HVC) // max(1, cfg.HVC // 2))
                return
            sq, rs = s.tmp(), s.tmp()
            P.op("act", [y.res[0]], [sq.res[0]], lambda: nc.scalar.activation(out=sq.t[:, 0:Tw], in_=y.t[:, 0:Tw], func=AF.Square))
            mb = s.misc_bank()
            P.op("pe", [sq.res[0]] + s.cres, [s.PSR[mb]], lambda: nc.tensor.matmul(
                s.PS[mb][:, 0:Tw], lhsT=s.ones_f, rhs=sq.t[:, 0:Tw], start=True, stop=True))
            P.op("act", [s.PSR[mb]] + s.cres, [rs.res[0]], lambda: nc.scalar.activation(
                out=rs.t[:, 0:Tw], in_=s.PS[mb][:, 0:Tw], func=AF.Sqrt, bias=s.eps6, scale=1.0))
            P.op("dve", [rs.res[0]], [rs.res[0]], lambda: nc.vector.reciprocal(out=rs.t[:, 0:Tw], in_=rs.t[:, 0:Tw]))
            P.op("dve", [y.res[0], rs.res[0]], [y.res[0]], lambda: nc.vector.tensor_tensor(
                out=y.t[:, 0:Tw], in0=y.t[:, 0:Tw], in1=rs.t[:, 0:Tw], op=ALU.mult))
            store(y, qk_dst(oc), 0 if oc < GHK_ else 1)

        def evac_z(oc, ps, psr):
            y = s.tmp()
            P.op("act", [psr], [y.res[0]], lambda: nc.scalar.activation(out=y.t[:, 0:Tw], in_=ps, func=AF.Silu))
            store(y, zv_[:, oc - NQK, o0:o0 + nm])

        s.gemm(Win, KC, hin[0], hin[1], list(range(NQK)), evac_qkv, Tw)
        s.gemm(Win, KC, hin[0], hin[1], list(range(NQK, NQK + cfg.GHV)), evac_z, Tw)

        GHV = cfg.GHV

        def evac_b(oc, ps, psr):
            o = s.tmp()
            P.op("act", [psr], [o.res[0]], lambda: nc.scalar.activation(out=o.t[0:GHV, 0:Tw], in_=ps, func=AF.Sigmoid))
            dres = Res()
            P.dma("sp", o.dsem, [o.res[0]], [dres], b_dst, o.t[0:GHV, c0:c0 + nm])
            out_res.append(dres)
            if s.fz:
                s.grp_res[ti][4].append(dres)

        def evac_a(oc, ps, psr):
            o, t, t2 = s.tmp(), s.tmp(), s.tmp()
            R_ = slice(0, GHV)
            P.op("dve", [psr, s.GS.res[0]], [t.res[0]], lambda: nc.vector.tensor_scalar(
                out=t.t[R_, 0:Tw], in0=ps, scalar1=s.GS.t[R_, 0:1], scalar2=None, op0=ALU.add))
            P.op("dve", [t.res[0]], [t2.res[0]], lambda: nc.vector.scalar_tensor_tensor(
                out=t2.t[R_, 0:Tw], in0=t.t[R_, 0:Tw], scalar=-1.0, in1=t.t[R_, 0:Tw], op0=ALU.mult, op1=ALU.min))
            P.op("act", [t2.res[0]], [t2.res[0]], lambda: nc.scalar.activation(out=t2.t[R_, 0:Tw], in_=t2.t[R_, 0:Tw], func=AF.Exp))
            P.op("act", [t2.res[0]] + s.cres, [t2.res[0]], lambda: nc.scalar.activation(
                out=t2.t[R_, 0:Tw], in_=t2.t[R_, 0:Tw], func=AF.Ln, bias=s.EPS.t[R_, 2:3], scale=1.0))
            P.op("dve", [t.res[0], t2.res[0]], [t.res[0]], lambda: nc.vector.scalar_tensor_tensor(
                out=t.t[R_, 0:Tw], in0=t.t[R_, 0:Tw], scalar=0.0, in1=t2.t[R_, 0:Tw], op0=ALU.max, op1=ALU.add))
            P.op("dve", [t.res[0], s.GS.res[0]], [o.res[0]], lambda: nc.vector.tensor_scalar(
                out=o.t[R_, 0:Tw], in0=t.t[R_, 0:Tw], scalar1=s.GS.t[R_, 1:2], scalar2=None, op0=ALU.mult))
            dres = Res()
            P.dma("sp", o.dsem, [o.res[0]], [dres], a_dst, o.t[0:GHV, c0:c0 + nm])
            out_res.append(dres)
            if s.fz:
                s.grp_res[ti][4].append(dres)

        nq = (NQK + cfg.GHV) * 128
        s.gemm(Win[:, nq:nq + GHV], KC, hin[0], hin[1], [0], evac_b, Tw, cw=GHV)
        s.gemm(Win[:, nq + GHV:nq + 2 * GHV], KC, hin[0], hin[1], [0], evac_a, Tw, cw=GHV)


NBM = 18


class StageB(KB):
    def build(s):
        cfg, nc = s.cfg, s.nc
        HVC, HKC, NCH, SEQ = cfg.HVC, cfg.HKC, cfg.NCH, cfg.SEQ
        REP = HVC // HKC
        s.setup(need_xh=False)
        P = s.P
        NCOL = NCH * HVC
        NCR, T, NBK = cfg.NCORE, cfg.T, cfg.NB
        CPR = cfg.MAIN // 128
        if s.fz:
            cid = s.parent.cid
            gout = s.over["gout"]
            HVh = HVC // 2
            mine = s.over["mine"]
            mres = Res()
            for ti in range(cfg.NT):
                for g in range(4):
                    gv = gout[ti][g].rearrange("(r c x) t -> r c (x t)", r=NCR, c=NCR)
                    mv = mine[ti][g].rearrange("(r x) t -> r (x t)", r=NCR)
                    src = gv[:, bass.ds(cid, 1), :].rearrange("r o e -> r (o e)")
                    P.dma("sp", s.iosem[(ti * 4 + g) % 4], [], [mres], mv, src)
            g4 = [gout[ti][4].rearrange("(r y) t -> r y t", r=NCR) for ti in range(cfg.NT)]
            XQ, XV = HKC * 128, HVh * 128

            def loc(n):
                r, bl = n // CPR, n % CPR + 1
                return r, bl // NBK, (bl % NBK) * 128

            def q_src(n, hk, which):
                r, ti, c0_ = loc(n)
                return mine[ti][which][r * XQ + hk * 128:r * XQ + (hk + 1) * 128, c0_:c0_ + 128]

            def v_src(n, hv):
                r, ti, c0_ = loc(n)
                hh = hv % HVh
                return mine[ti][2 + hv // HVh][r * XV + hh * 128:r * XV + (hh + 1) * 128, c0_:c0_ + 128]
            o_loc = s.over["o_loc"]

            def o_dst(n, hv):
                return o_loc[n // CPR][(n % CPR) * 128:(n % CPR + 1) * 128, hv * 128:(hv + 1) * 128]
        else:
            qT = s.din("qT", [HKC * 128, SEQ])
            kT = s.din("kT", [HKC * 128, SEQ])
            vT = s.din("vT", [HVC * 128, SEQ])
            gtm = s.din("gtm", [128, NCOL])
            btm = s.din("btm", [128, NCOL])
            o_tm = s.dout("o_tm", [SEQ, HVC * 128])
            mres = Res()
            q_src = lambda n, hk, which: (qT, kT)[which][hk * 128:(hk + 1) * 128, n * 128:(n + 1) * 128]
            v_src = lambda n, hv: vT[hv * 128:(hv + 1) * 128, n * 128:(n + 1) * 128]
            o_dst = lambda n, hv: o_tm[n * 128:(n + 1) * 128, hv * 128:(hv + 1) * 128]
        s.BM = Buf(P, "bm", [128, NBM, 128], F32)
        P.dma("sp", s.ld, [], [s.BM.res[0]], s.BM.t[:], s.din("bmasks", [128, NBM, 128]))
        s.EPS = Buf(P, "eps", [128, 2], F32)
        P.op("dve", [], [s.EPS.res[0]], lambda: nc.vector.memset(s.EPS.t[:, 0:1], 1e-6))
        bm = lambda i: s.BM.t[:, i, :]
        BMR = s.BM.res[0]
        SC = Buf(P, "sc", [128, 8, NCOL], F32)
        G_, B_, GC, GL, EG, EKD, EGL, EGS = range(8)
        sc = lambda i: SC.t[:, i, :]
        scr = SC.res[0]
        if not s.fz:
            P.dma("sp", s.ld, [], [scr], sc(G_), gtm)
            P.dma("sp", s.ld, [], [scr], sc(B_), btm)
        else:
            GH2 = 2 * cfg.GHV
            GB = Buf(P, "gbrows", [GH2, SEQ], F32)
            SEL = Buf(P, "gsel", [GH2, 2 * HVC], F32)
            gsem = P.new_dma_sem()
            P.dma("sp", gsem, [], [SEL.res[0]], SEL.t[:], s.din("gsel", [GH2, 2 * HVC]))
            for r in range(NCR):
                for ti in range(cfg.NT):
                    cs = 128 if ti == 0 else 0
                    d0 = r * cfg.MAIN + ti * T + cs - 128
                    P.dma("sp", gsem, [], [GB.res[0]], GB.t[:, d0:d0 + T - cs], g4[ti][r, :, cs:T])
            for n in range(NCH):
                bk = n % 8
                P.op("pe", [GB.res[0], SEL.res[0]], [s.PSR[bk]], lambda: nc.tensor.matmul(
                    s.PS[bk][:, 0:2 * HVC], lhsT=GB.t[:, n * 128:(n + 1) * 128], rhs=SEL.t[:], start=True, stop=True))
                P.op("act", [s.PSR[bk]], [scr], lambda: nc.scalar.copy(out=SC.t[:, B_, n * HVC:(n + 1) * HVC], in_=s.PS[bk][:, 0:HVC]))
                P.op("act", [s.PSR[bk]], [scr], lambda: nc.scalar.copy(out=SC.t[:, G_, n * HVC:(n + 1) * HVC], in_=s.PS[bk][:, HVC:2 * HVC]))
        for c0 in range(0, NCOL, 512):
            cn = min(512, NCOL - c0)
            b1, b2 = 0, 1
            P.op("pe", [scr, BMR], [s.PSR[b1]], lambda: nc.tensor.matmul(s.PS[b1][:, 0:cn], lhsT=bm(0), rhs=SC.t[:, G_, c0:c0 + cn], start=True, stop=True))
            P.op("pe", [scr] + s.cres, [s.PSR[b2]], lambda: nc.tensor.matmul(s.PS[b2][:, 0:cn], lhsT=s.ones_f, rhs=SC.t[:, G_, c0:c0 + cn], start=True, stop=True))
            P.op("act", [s.PSR[b1]], [scr], lambda: nc.scalar.copy(out=SC.t[:, GC, c0:c0 + cn], in_=s.PS[b1][:, 0:cn]))
            P.op("act", [s.PSR[b2]], [scr], lambda: nc.scalar.copy(out=SC.t[:, GL, c0:c0 + cn], in_=s.PS[b2][:, 0:cn]))
        P.op("act", [scr], [scr], lambda: nc.scalar.activation(out=sc(EG), in_=sc(GC), func=AF.Exp))
        P.op("act", [scr], [scr], lambda: nc.scalar.activation(out=sc(EGL), in_=sc(GL), func=AF.Exp))
        P.op("dve", [scr], [scr], lambda: nc.vector.tensor_tensor(out=sc(EKD), in0=sc(GL), in1=sc(GC), op=ALU.subtract))
        P.op("act", [scr], [scr], lambda: nc.scalar.activation(out=sc(EKD), in_=sc(EKD), func=AF.Exp))
        P.op("dve", [scr], [scr], lambda: nc.vector.tensor_scalar(out=sc(EGS), in0=sc(EG), scalar1=128.0 ** -0.5, scalar2=None, op0=ALU.mult))
        P.op("dve", [scr], [scr], lambda: nc.vector.tensor_scalar(out=sc(G_), in0=sc(B_), scalar1=-1.0, scalar2=None, op0=ALU.mult))
        NB_ = G_
        NU = 60
        upool = [[Buf(P, "u%d_%d" % (h, i), [128, 128], F32) for i in range(NU)] for h in range(HVC)]
        spool = [Buf(P, "sh%d" % i, [128, 128], F32) for i in range(24)]
        vpool = [Buf(P, "vt%d" % i, [128, 128], F32) for i in range(8)]
        opool = [Buf(P, "ot%d" % i, [128, 128], F32) for i in range(8)]
        for b in vpool + opool:
            b.dsem = P.new_dma_sem()
        cnt = {"s": 0, "b": 0, "v": 0, "o": 0}
        ucnt = [0] * HVC
        cur = [0]

        def ut():
            h = cur[0]
            b = upool[h][ucnt[h] % NU]
            ucnt[h] += 1
            return b

        def sht():
            b = spool[cnt["s"] % 24]
            cnt["s"] += 1
            return b

        def bank():
            b = cnt["b"] % 8
            cnt["b"] += 1
            return b

        hb = [0] * HVC
        assert HVC <= 4

        def hbank():
            h = cur[0]
            b = 2 * h + (hb[h] % 2)
            hb[h] += 1
            return b

        ldsem = [P.new_dma_sem() for _ in range(4)]
        S = [[Buf(P, "S%d_%d" % (h, i), [128, 128], F32) for i in range(2)] for h in range(HVC)]
        for h in range(HVC):
            P.op("dve", [], [S[h][0].res[0]], lambda h=h: nc.vector.memset(S[h][0].t[:], 0.0))
        out_res = []
        identr = s.cres

        def mm(out_bank, lhsT, rhs, reads):
            P.op("pe", reads, [s.PSR[out_bank]], lambda: nc.tensor.matmul(s.PS[out_bank][:, 0:128], lhsT=lhsT, rhs=rhs, start=True, stop=True))

        def tr(out_bank, in_, reads):
            P.op("pe", reads + identr, [s.PSR[out_bank]], lambda: nc.tensor.transpose(out=s.PS[out_bank][:, 0:128], in_=in_, identity=s.ident_f))

        def cp(dst, bk):
            P.op("act", [s.PSR[bk]], [dst.res[0]], lambda: nc.scalar.copy(out=dst.t[:], in_=s.PS[bk][:, 0:128]))

        def tt(dst, a, ar, b_, br, op):
            P.op("dve", ar + br, [dst.res[0]], lambda: nc.vector.tensor_tensor(out=dst.t[:], in0=a, in1=b_, op=op))

        def stt(dst, a, ar, scal, b_, br, op0, op1):
            P.op("dve", ar + br + [scr], [dst.res[0]], lambda: nc.vector.scalar_tensor_tensor(
                out=dst.t[:], in0=a, scalar=scal, in1=b_, op0=op0, op1=op1))

        def shared(n, hk):
            QT, KT = sht(), sht()
            sem = ldsem[n % 4]
            P.dma("sp", sem, [mres], [QT.res[0]], QT.t[:], q_src(n, hk, 0))
            P.dma("sp", sem, [mres], [KT.res[0]], KT.t[:], q_src(n, hk, 1))
            b1, b2, b3 = bank(), bank(), bank()
            mm(b1, KT.t[:], KT.t[:], [KT.res[0]])
            mm(b2, KT.t[:], QT.t[:], [KT.res[0], QT.res[0]])
            tr(b3, KT.t[:], [KT.res[0]])
            Gs, ARs, Ktm = sht(), sht(), sht()
            cp(Gs, b1)
            cp(ARs, b2)
            cp(Ktm, b3)
            return QT, KT, Gs, ARs, Ktm

        def unit(n, hv, sh, par):
            QT, KT, Gs, ARs, Ktm = sh
            col = n * HVC + hv
            c1 = lambda i: SC.t[:, i, col:col + 1]
            VT = vpool[cnt["v"] % 8]
            cnt["v"] += 1
            P.dma("sp", VT.dsem, [mres], [VT.res[0]], VT.t[:], v_src(n, hv))
            bv = hbank()
            tr(bv, VT.t[:], [VT.res[0]])
            Vtm = ut()
            cp(Vtm, bv)
            dg = ut()
            P.op("dve", identr + [scr], [dg.res[0]], lambda: nc.vector.tensor_scalar(
                out=dg.t[:], in0=s.ident_f, scalar1=c1(GC), scalar2=None, op0=ALU.mult))
            bR = hbank()
            mm(bR, s.ones_f, dg.t[:], [dg.res[0]] + identr)
            yield
            DnS, Dt = ut(), ut()
            stt(DnS, s.PS[bR][:, 0:128], [s.PSR[bR]], c1(GC), bm(1), [BMR], ALU.subtract, ALU.max)
            stt(Dt, s.PS[bR][:, 0:128], [s.PSR[bR]], c1(GC), bm(2), [BMR], ALU.subtract, ALU.min)
            P.op("act", [DnS.res[0]], [DnS.res[0]], lambda: nc.scalar.activation(out=DnS.t[:], in_=DnS.t[:], func=AF.Exp, scale=-1.0))
            P.op("act", [Dt.res[0]], [Dt.res[0]], lambda: nc.scalar.activation(out=Dt.t[:], in_=Dt.t[:], func=AF.Exp))
            An, At, attnT = ut(), ut(), ut()
            stt(An, Gs.t[:], [Gs.res[0]], c1(B_), DnS.t[:], [DnS.res[0]], ALU.mult, ALU.mult)
            bA = hbank()
            tr(bA, An.t[:], [An.res[0]])
            stt(attnT, ARs.t[:], [ARs.res[0]], 128.0 ** -0.5, Dt.t[:], [Dt.res[0]], ALU.mult, ALU.mult)
            cp(At, bA)
            yield
            E, Et = ut(), ut()
            tt(E, An.t[:], [An.res[0]], bm(3), [BMR], ALU.mult)
            tt(Et, At.t[:], [At.res[0]], bm(10), [BMR], ALU.mult)
            Tb, Tt = ut(), ut()
            tt(Tb, s.ident_f, identr, E.t[:], [E.res[0]], ALU.subtract)
            tt(Tt, s.ident_f, identr, Et.t[:], [Et.res[0]], ALU.subtract)
            for l in range(1, 7):
                last = (l == 6)
                E, Et = ut(), ut()
                tt(E, An.t[:], [An.res[0]], bm(3 + l), [BMR], ALU.mult)
                tt(Et, At.t[:], [At.res[0]], bm(10 + l), [BMR], ALU.mult)
                yield
                bM, bM2 = hbank(), hbank()
                if not last:
                    mm(bM, Et.t[:], Tb.t[:], [Et.res[0], Tb.res[0]])
                mm(bM2, E.t[:], Tt.t[:], [E.res[0], Tt.res[0]])
                M, M2 = ut(), ut()
                if not last:
                    cp(M, bM)
                cp(M2, bM2)
                yield
                bT, bT2 = hbank(), hbank()
                if not last:
                    mm(bT, Tt.t[:], M.t[:], [Tt.res[0], M.res[0]])
                mm(bT2, Tb.t[:], M2.t[:], [Tb.res[0], M2.res[0]])
                Tb2, Tt2 = ut(), ut()
                if not last:
                    tt(Tb2, Tb.t[:], [Tb.res[0]], s.PS[bT][:, 0:128], [s.PSR[bT]], ALU.subtract)
                tt(Tt2, Tt.t[:], [Tt.res[0]], s.PS[bT2][:, 0:128], [s.PSR[bT2]], ALU.subtract)
                Tb, Tt = Tb2, Tt2
                yield
            Sc, Sn = S[hv][par], S[hv][1 - par]
            bK = hbank()
            mm(bK, KT.t[:], Sc.t[:], [KT.res[0], Sc.res[0]])
            bQ = hbank()
            mm(bQ, QT.t[:], Sc.t[:], [QT.res[0], Sc.res[0]])
            t = ut()
            stt(t, s.PS[bK][:, 0:128], [s.PSR[bK]], c1(EG), Vtm.t[:], [Vtm.res[0]], ALU.mult, ALU.subtract)
            QSs = ut()
            cp(QSs, bQ)
            rv = ut()
            P.op("dve", [t.res[0], scr], [rv.res[0]], lambda: nc.vector.tensor_scalar(
                out=rv.t[:], in0=t.t[:], scalar1=c1(NB_), scalar2=None, op0=ALU.mult))
            yield
            bN = hbank()
            mm(bN, Tt.t[:], rv.t[:], [Tt.res[0], rv.res[0]])
            vn, vne = ut(), ut()
            cp(vn, bN)
            P.op("dve", [vn.res[0], scr], [vne.res[0]], lambda: nc.vector.tensor_scalar(
                out=vne.t[:], in0=vn.t[:], scalar1=c1(EKD), scalar2=None, op0=ALU.mult))
            yield
            bAV, bKV = hbank(), hbank()
            mm(bAV, attnT.t[:], vn.t[:], [attnT.res[0], vn.res[0]])
            mm(bKV, Ktm.t[:], vne.t[:], [Ktm.res[0], vne.res[0]])
            AVs = ut()
            cp(AVs, bAV)
            stt(Sn, Sc.t[:], [Sc.res[0]], c1(EGL), s.PS[bKV][:, 0:128], [s.PSR[bKV]], ALU.mult, ALU.add)
            o = ut()
            stt(o, QSs.t[:], [QSs.res[0]], c1(EGS), AVs.t[:], [AVs.res[0]], ALU.mult, ALU.add)
            yield
            sq, st = ut(), ut()
            P.op("act", [o.res[0]], [sq.res[0], st.res[0]], lambda: nc.scalar.activation(
                out=sq.t[:], in_=o.t[:], func=AF.Square, accum_out=st.t[:, 0:1]))
            P.op("act", [st.res[0], s.EPS.res[0]], [st.res[0]], lambda: nc.scalar.activation(
                out=st.t[:, 1:2], in_=st.t[:, 0:1], func=AF.Sqrt, scale=1.0 / 128, bias=s.EPS.t[:, 0:1]))
            P.op("dve", [st.res[0]], [st.res[0]], lambda: nc.vector.reciprocal(out=st.t[:, 2:3], in_=st.t[:, 1:2]))
            on = opool[cnt["o"] % 8]
            cnt["o"] += 1
            P.op("dve", [o.res[0], st.res[0], BMR], [on.res[0]], lambda: nc.vector.scalar_tensor_tensor(
                out=on.t[:], in0=o.t[:], scalar=st.t[:, 2:3], in1=bm(17), op0=ALU.mult, op1=ALU.mult))
            dres = Res()
            P.dma("sp", on.dsem, [on.res[0]], [dres], o_dst(n, hv), on.t[:])
            out_res.append(dres)

        for n in range(NCH):
            shs = [shared(n, hk) for hk in range(HKC)]
            gens = [(hv, unit(n, hv, shs[hv // REP], n % 2)) for hv in range(HVC)]
            while gens:
                for hv_, g in list(gens):
                    cur[0] = hv_
                    try:
                        next(g)
                    except StopIteration:
                        gens.remove((hv_, g))
            if s.fz and n % CPR == CPR - 1:
                p = n // CPR
                cres = Res()
                P.coll_allgather(NCR, list(out_res), [cres], o_loc[p].opt(), s.over["o_gat"][p].opt())
                out_res.clear()
                big = s.over["big"]
                gv = s.over["o_gat"][p].rearrange("(a q) c -> q a c", q=128)
                bv = big[p + 1].rearrange("(a q) c -> q a c", q=128)
                na = NCR * cfg.MAIN // 128
                for a0 in range(0, na, 8):
                    dres = Res()
                    P.dma("sp", s.iosem[p % 4], [cres], [dres], bv[:, a0:a0 + 8, :], gv[:, a0:a0 + 8, :])
                    s.parent.big_res.append(dres)
        s.finish(out_res)
        return s


class Mega(KB):
    def build(s):
        cfg, nc = s.cfg, s.nc
        NCR, T, NT = cfg.NCORE, cfg.T, cfg.NT
        P = s.P = Prog(nc, s.es)
        s.PS = [P.ps("ps%d" % i, [128, 512]) for i in range(8)]
        s.PSR = [Res(excl=True) for _ in range(8)]
        s.cid = nc.sync.partition_id()
        s.big_res = []
        dt = lambda name, shape: nc.dram_tensor(name, list(shape), F32).ap()
        x2s = dt("x2s", [cfg.D, cfg.TOK])
        zss = dt("zss", [cfg.GVD, cfg.TOK])
        HVh = cfg.HVC // 2
        rows = [cfg.GKD, cfg.GKD, NCR * HVh * 128, NCR * HVh * 128, 2 * cfg.GHV]
        gin = [[dt("gin%d_%d" % (ti, g), [rows[g], T]) for g in range(5)] for ti in range(NT)]
        gout = [[dt("gout%d_%d" % (ti, g), [NCR * rows[g], T]) for g in range(5)] for ti in range(NT)]
        o_loc = [dt("oloc%d" % p, [cfg.MAIN, cfg.HVC * 128]) for p in range(NCR)]
        o_gat = [dt("ogat%d" % p, [NCR * cfg.MAIN, cfg.HVC * 128]) for p in range(NCR)]
        big = dt("obig", [NCR + 1, NCR * cfg.MAIN, cfg.HVC * 128])
        xg = [cfg.HKC * 128, cfg.HKC * 128, HVh * 128, HVh * 128]
        mine = [[dt("mine%d_%d" % (ti, g), [NCR * xg[g], T]) for g in range(4)] for ti in range(NT)]
        mybig = nc.dram_tensor("mybig", [NCR, cfg.TOK, cfg.HVC * 128], F32).ap()
        def gather_tile(A_, ti):
            for g in range(5):
                P.coll_allgather(NCR, A_.grp_res[ti][g], [Res()], gin[ti][g].opt(), gout[ti][g].opt())
        s.gather_tile = gather_tile
        A = StageA(cfg, "a", parent=s)
        A.over = {"x2s": x2s, "zss": zss, "gin": gin}
        with A.es:
            A.build()
            z = Buf(P, "zz", [128, cfg.HVC * 128], F32)
            P.op("dve", [], [z.res[0]], lambda: nc.vector.memset(z.t[:], 0.0))
            bz = big[0].rearrange("(r q) c -> r q c", r=NCR)
            for r in range(NCR):
                dres = Res()
                P.dma("sp", A.iosem[0], [z.res[0]], [dres], bz[r, cfg.MAIN - 128:cfg.MAIN, :], z.t[:])
            s.gather_tile(A, NT - 1)
            P.barrier()
        B = StageB(cfg, "b", parent=s)
        B.over = {"gout": gout, "o_loc": o_loc, "o_gat": o_gat, "big": big, "mine": mine}
        with B.es:
            B.build()
            P.barrier()
        C = StageC(cfg, "c", parent=s)
        C.over = {"x2T": x2s, "zsT": zss, "big": big, "mybig": mybig}
        with C.es:
            C.build()
        return s


def fm(v, n=128):
    v = np.asarray(v, np.float32)
    return np.ascontiguousarray(v.reshape(-1, n).T)


def make_consts():
    c = np.zeros((128, 4, 128), np.float32)
    c[:, 0, :] = 1.0
    c[:, 1, :] = np.eye(128, dtype=np.float32)
    for m in range(128):
        c[(m + 64) % 128, 2, m] = 1.0
    return c


def pack_vecs(cfg, inp):
    V = np.zeros((128, NVEC(cfg), cfg.KC), np.float32)
    for l in range(4):
        V[:, V_MIX0 + 2 * l] = fm(inp["mix_norm"][l])
        V[:, V_FFN0 + 2 * l] = fm(inp["ffn_norm"][l])
    V[:, V_FINAL] = fm(inp["final_norm"])
    D = cfg.D
    V[:, V_CB1V] = fm(inp["c_b_pw1"][0][:D])
    V[:, V_CB1G] = fm(inp["c_b_pw1"][0][D:])
    V[:, V_CBDW] = fm(inp["c_b_dw"][0])
    V[:, V_CLNG] = fm(inp["c_ln_g"][0])
    V[:, V_CLNB] = fm(inp["c_ln_b"][0])
    V[:, V_CB2] = fm(inp["c_b_pw2"][0])
    for k in range(3):
        V[:, V_DCONV + k] = fm(inp["d_w_conv"][0][k])
    for k in range(cfg.CK):
        V[:, V_CDW + k] = fm(inp["c_w_dw"][0][k])
    return V


def pack_fvecs(cfg, inp, layers):
    Fv = np.zeros((128, 4 * len(layers), cfg.FC), np.float32)
    for i, l in enumerate(layers):
        for k in range(3):
            Fv[:, 4 * i + k] = fm(inp["f_w_conv"][l][k])
        Fv[:, 4 * i + 3] = fm(inp["f_b_conv"][l])
    return Fv


def halo_slices(cfg, aT, c):
    s0 = c * cfg.MAIN - cfg.HALO
    if s0 >= 0:
        return np.ascontiguousarray(aT[:, s0:s0 + cfg.TOK])
    out = np.zeros((aT.shape[0], cfg.TOK), aT.dtype)
    out[:, -s0:] = aT[:, 0:cfg.TOK + s0]
    return out


def hv_arr(c):
    return np.full((128, 1), 0.0 if c == 0 else 1.0, np.float32)


def run_stage_c(cfg, inp, x2T, onT, zsT, trace=False):
    st = StageC(cfg, "c")
    with st.es:
        st.build()
    V = pack_vecs(cfg, inp)
    Fv = pack_fvecs(cfg, inp, [1, 2, 3])
    cst = make_consts()
    maps = []
    for c in range(cfg.NCORE):
        m = {"consts": cst, "vecs": V, ("fvecs_%d" % (Fv.shape[1] // 4)): Fv, "hv_in": hv_arr(c),
             "x2T": halo_slices(cfg, x2T, c), "onT": halo_slices(cfg, onT, c), "zsT": halo_slices(cfg, zsT, c),
             "b_w_o": inp["b_w_o"][0], "c_w_pw1": inp["c_w_pw1"][0], "c_w_pw2": inp["c_w_pw2"][0],
             "d_w_in": inp["d_w_in"][0], "d_w_out": inp["d_w_out"][0]}
        for l in (1, 2, 3):
            m["wg%d" % l] = inp["f_w_gate"][l]
            m["wu%d" % l] = inp["f_w_up"][l]
            m["wd%d" % l] = inp["f_w_down"][l]
        maps.append(m)
    res = run_bass_kernel_spmd(st.nc, maps, core_ids=list(range(cfg.NCORE)), trace=trace)
    outT = np.concatenate([r["outT"] for r in res.results], axis=1)
    return outT, res


def rope_tables(cfg, c):
    n = 128 + cfg.TOK
    pos = (c * cfg.MAIN - 256 + np.arange(n)).astype(np.float32)
    inv = np.power(np.float32(10000.0), -np.arange(64, dtype=np.float32) / np.float32(64)).astype(np.float32)
    ang = (pos[None, :] * inv[:, None]).astype(np.float32)
    cs, sn = np.cos(ang).astype(np.float32), np.sin(ang).astype(np.float32)
    tab = np.zeros((128, 2, n), np.float32)
    tab[0:64, 0], tab[64:128, 0] = cs, cs
    tab[0:64, 1], tab[64:128, 1] = -sn, sn
    tab[:, :, pos < 0] = 0.0
    return tab


def attn_masks(c):
    q = np.arange(128)[:, None]
    k = np.arange(128)[None, :]
    NEG = np.float32(-30000.0)
    m = np.zeros((128, 2, 256), np.float32)
    prev = np.where(k > q, 0.0, NEG).astype(np.float32)
    cur = np.where(k <= q, 0.0, NEG).astype(np.float32)
    m[:, 0, 0:128], m[:, 0, 128:256] = prev, cur
    m[:, 1, 0:128], m[:, 1, 128:256] = (NEG if c == 0 else prev), cur
    return m


def run_stage_a(cfg, inp, xT, trace=False):
    st = StageA(cfg, "a")
    with st.es:
        st.build()
    V = pack_vecs(cfg, inp)
    Fv = pack_fvecs(cfg, inp, [0])
    cst = make_consts()
    NQK = 2 * cfg.GHK + cfg.GHV
    gconv = np.stack([fm(inp["b_conv"][0][k]) for k in range(4)], axis=1)
    gsc = np.stack([inp["b_dt_bias"][0], inp["b_a_log"][0]], axis=1).astype(np.float32)
    sinkb = np.ascontiguousarray(np.broadcast_to(inp["a_sinks"][0][None, :], (128, cfg.AH))).astype(np.float32)
    maps = []
    for c in range(cfg.NCORE):
        s0 = c * cfg.MAIN - 256
        n = 128 + cfg.TOK
        xc = np.zeros((cfg.D, n), np.float32)
        lo = max(0, s0)
        xc[:, lo - s0:] = xT[:, lo:s0 + n]
        maps.append({"consts": cst, "vecs": V, ("fvecs_%d" % (Fv.shape[1] // 4)): Fv, "hv_in": hv_arr(c), "xT": xc,
                     "a_w_qkv": inp["a_w_qkv"][0], "a_w_o": inp["a_w_o"][0], "wg0": inp["f_w_gate"][0],
                     "wu0": inp["f_w_up"][0], "wd0": inp["f_w_down"][0], "b_w_in": inp["b_w_in"][0],
                     "rope_tab": rope_tables(cfg, c), "masks": attn_masks(c), "sinkb": sinkb, "gconv": gconv, "gsc": gsc})
    res = run_bass_kernel_spmd(st.nc, maps, core_ids=list(range(cfg.NCORE)), trace=trace)
    out = {k: np.concatenate([r[k] for r in res.results], axis=1) for k in ("x2T", "qkT", "vT", "zsT", "gbT")}
    return out, res


def gdn_masks(norm_w):
    bmk = np.zeros((128, NBM, 128), np.float32)
    i = np.arange(128)[:, None]
    j = np.arange(128)[None, :]
    bmk[:, 0] = (i <= j)
    bmk[:, 1] = np.where(i > j, 0.0, 1e4)
    bmk[:, 2] = np.where(j >= i, 0.0, -1e4)
    for l in range(7):
        b = 1 << l
        mN = ((i // (2 * b)) == (j // (2 * b))) & (((i // b) % 2) == 1) & (((j // b) % 2) == 0)
        bmk[:, 3 + l] = mN
        bmk[:, 10 + l] = mN.T
    bmk[:, 17] = np.broadcast_to(np.asarray(norm_w, np.float32)[None, :], (128, 128))
    return bmk


def run_stage_b(cfg, inp, qkT, vT, gbT, trace=False):
    st = StageB(cfg, "b")
    with st.es:
        st.build()
    cst = make_consts()
    bmk = gdn_masks(inp["b_norm"][0])
    HVC, HKC, NCH, GHK, GHV = cfg.HVC, cfg.HKC, cfg.NCH, cfg.GHK, cfg.GHV
    maps = []
    for c in range(cfg.NCORE):
        qs = qkT[(c * HKC) * 128:((c + 1) * HKC) * 128]
        ks = qkT[(GHK + c * HKC) * 128:(GHK + (c + 1) * HKC) * 128]
        vs = vT[(c * HVC) * 128:((c + 1) * HVC) * 128]
        bt = gbT[c * HVC:(c + 1) * HVC]
        gt = gbT[GHV + c * HVC:GHV + (c + 1) * HVC]
        tm = lambda a: np.ascontiguousarray(a.reshape(HVC, NCH, 128).transpose(2, 1, 0).reshape(128, NCH * HVC))
        maps.append({"consts": cst, "bmasks": bmk, "qT": np.ascontiguousarray(qs), "kT": np.ascontiguousarray(ks),
                     "vT": np.ascontiguousarray(vs), "gtm": tm(gt), "btm": tm(bt)})
    res = run_bass_kernel_spmd(st.nc, maps, core_ids=list(range(cfg.NCORE)), trace=trace)
    o = np.concatenate([r["o_tm"] for r in res.results], axis=1)
    return o, res


FUSED = True
def run_mega(cfg, inp, xT, trace=False):
    mg = Mega(cfg, "m")
    with mg.es:
        mg.build()
    V = pack_vecs(cfg, inp)
    FvA = pack_fvecs(cfg, inp, [0])
    FvC = pack_fvecs(cfg, inp, [1, 2, 3])
    cst = make_consts()
    gconv = np.stack([fm(inp["b_conv"][0][k]) for k in range(4)], axis=1)
    gsc = np.stack([inp["b_dt_bias"][0], inp["b_a_log"][0]], axis=1).astype(np.float32)
    sinkb = np.ascontiguousarray(np.broadcast_to(inp["a_sinks"][0][None, :], (128, cfg.AH))).astype(np.float32)
    bmk = gdn_masks(inp["b_norm"][0])
    maps = []
    for c in range(cfg.NCORE):
        s0 = c * cfg.MAIN - 256
        n = 128 + cfg.TOK
        xc = np.zeros((cfg.D, n), np.float32)
        lo = max(0, s0)
        xc[:, lo - s0:] = xT[:, lo:s0 + n]
        m = {"consts": cst, "vecs": V, "fvecs_1": FvA, "fvecs_3": FvC, "hv_in": hv_arr(c), "xT": xc,
             "a_w_qkv": inp["a_w_qkv"][0], "a_w_o": inp["a_w_o"][0], "wg0": inp["f_w_gate"][0],
             "wu0": inp["f_w_up"][0], "wd0": inp["f_w_down"][0], "b_w_in": inp["b_w_in"][0],
             "rope_tab": rope_tables(cfg, c), "masks": attn_masks(c), "sinkb": sinkb, "gconv": gconv, "gsc": gsc,
             "bmasks": bmk, "gsel": gsel_arr(cfg, c),
             "b_w_o": inp["b_w_o"][0], "c_w_pw1": inp["c_w_pw1"][0], "c_w_pw2": inp["c_w_pw2"][0],
             "d_w_in": inp["d_w_in"][0], "d_w_out": inp["d_w_out"][0]}
        for l in (1, 2, 3):
            m["wg%d" % l] = inp["f_w_gate"][l]
            m["wu%d" % l] = inp["f_w_up"][l]
            m["wd%d" % l] = inp["f_w_down"][l]
        maps.append(m)
    res = run_bass_kernel_spmd(mg.nc, maps, core_ids=list(range(cfg.NCORE)), trace=trace)
    outT = np.concatenate([r["outT"] for r in res.results], axis=1)
    return outT, res


def kernel(**inputs):
    inp = {k: np.asarray(v) for k, v in inputs.items()}
    cfg = Cfg()
    if FUSED:
        xT = np.ascontiguousarray(inp["x"][0].T)
        outT, _ = run_mega(cfg, inp, xT)
        return np.ascontiguousarray(outT.T)[None].astype(np.float32)
    xT = np.ascontiguousarray(inp["x"][0].T)
    a, _ = run_stage_a(cfg, inp, xT)
    o_tm, _ = run_stage_b(cfg, inp, a["qkT"], a["vT"], a["gbT"])
    onT = np.ascontiguousarray(o_tm.T)
    outT, _ = run_stage_c(cfg, inp, a["x2T"], onT, a["zsT"])
    return np.ascontiguousarray(outT.T)[None].astype(np.float32)


def gsel_arr(cfg, c):
    m = np.zeros((2 * cfg.GHV, 2 * cfg.HVC), np.float32)
    for j in range(cfg.HVC):
        m[c * cfg.HVC + j, j] = 1.0
        m[cfg.GHV + c * cfg.HVC + j, cfg.HVC + j] = 1.0
    return m
```

```python
import numpy as np
import concourse.bass as bass
import concourse.mybir as mybir
from concourse.bass_utils import run_bass_kernel_spmd
from contextlib import ExitStack

F32 = mybir.dt.float32
BF16 = mybir.dt.bfloat16
ALU = mybir.AluOpType
AF = mybir.ActivationFunctionType
AX = mybir.AxisListType


class Res:
    __slots__ = ("w", "r", "excl")

    def __init__(self, excl=False):
        self.w = None
        self.r = {}
        self.excl = excl


class Prog:
    ENG = ("pe", "act", "dve", "pool", "sp")

    def __init__(self, nc, es):
        self.nc = nc
        self.es = es
        self.es_sem = es
        self.n_cc = 0
        self.h = {"pe": nc.tensor, "act": nc.scalar, "dve": nc.vector, "pool": nc.gpsimd, "sp": nc.sync}
        self.sems = {}
        self.cnt = {}
        for k in self.ENG:
            self.sems[k] = es.enter_context(nc.semaphore("s_" + k))
            self.cnt[k] = 0
        self.seen = {k: {} for k in self.ENG}
        self.engkey = {k: k for k in self.ENG}
        self.gen = 0
        self.n_dma_sem = 0
        self.lastwait = {}
        self.serial = {}

    def sb(self, name, shape, dt):
        return self.es.enter_context(self.nc.sbuf_tensor(getattr(self, "prefix", "") + name, list(shape), dt))

    def ps(self, name, shape, dt=F32):
        return self.es.enter_context(self.nc.psum_tensor(name, list(shape), dt))

    def new_dma_sem(self):
        k = "dma%d" % self.n_dma_sem
        self.n_dma_sem += 1
        self.sems[k] = self.es_sem.enter_context(self.nc.semaphore(k))
        self.cnt[k] = 0
        return k

    def barrier(self):
        for e in self.ENG:
            for k, v in list(self.cnt.items()):
                if v > 0:
                    self._wait(e, (k, v))
        self.gen += 1
        for e in self.ENG:
            k = "%s_%d" % (e, self.gen)
            self.sems[k] = self.es_sem.enter_context(self.nc.semaphore("s_" + k))
            self.cnt[k] = 0
            self.engkey[e] = k

    def coll_allgather(self, ncore, reads, writes, in_ap, out_ap):
        k = "cc"
        self.n_cc += 1
        if k not in self.sems:
            self.sems[k] = self.es_sem.enter_context(self.nc.semaphore(k))
            self.cnt[k] = 0
        self._deps("pool", reads, writes)
        self.h["pool"].collective_compute("AllGather", ALU.bypass, replica_groups=[list(range(ncore))],
                                          ins=[in_ap], outs=[out_ap]).then_inc(self.sems[k])
        self.cnt[k] += 1
        self._commit((k, self.cnt[k]), reads, writes)

    def _wait(self, eng, ev):
        if ev is None:
            return
        key, val = ev
        if eng == "pe" and key.split("_")[0] == "pe":
            return
        if key.startswith("dma"):
            val = self.cnt[key]
        if self.seen[eng].get(key, 0) >= val:
            return
        self.h[eng].wait_ge(self.sems[key], val)
        self.seen[eng][key] = val
        if key.startswith("dma") and self.lastwait.get(key, 0) < val:
            self.lastwait[key] = val

    def _deps(self, eng, reads, writes):
        for r in reads:
            self._wait(eng, r.w)
        for w in writes:
            self._wait(eng, w.w)
            for k, v in w.r.items():
                self._wait(eng, (k, v))

    def _commit(self, ev, reads, writes):
        k, v = ev
        for r in reads:
            if r.r.get(k, 0) < v:
                r.r[k] = v
        for w in writes:
            w.w = ev
            w.r = {}

    def op(self, eng, reads, writes, fn):
        ex = [r for r in reads if r.excl and r not in writes]
        if ex:
            writes = list(writes) + ex
        self._deps(eng, reads, writes)
        k = self.engkey[eng]
        fn().then_inc(self.sems[k], 1)
        self.cnt[k] += 1
        self._commit((k, self.cnt[k]), reads, writes)

    def dma(self, q, semkey, reads, writes, out, in_, **kw):
        if self.lastwait.get(semkey, 0) > self.serial.get(semkey, 0):
            self._wait(q, (semkey, self.cnt[semkey]))
            self.serial[semkey] = self.cnt[semkey]
        self._deps(q, reads, writes)
        self.h[q].dma_start(out=out, in_=in_, **kw).then_inc(self.sems[semkey], 16)
        self.cnt[semkey] += 16
        self._commit((semkey, self.cnt[semkey]), reads, writes)

    def wait_all(self, eng, ress):
        for r in ress:
            self._wait(eng, r.w)


class Buf:
    def __init__(self, P, name, shape, dt, nres=1):
        self.t = P.sb(name, shape, dt)
        self.res = [Res() for _ in range(nres)]
        self.dsem = None


class Cfg:
    def __init__(s, D=4096, DFF=11008, AH=32, AKV=8, GHK=16, GHV=32, T=384, NT=3, NCORE=8, CK=31):
        s.D, s.DFF, s.AH, s.AKV, s.GHK, s.GHV, s.T, s.NT, s.NCORE, s.CK = D, DFF, AH, AKV, GHK, GHV, T, NT, NCORE, CK
        s.KC = D // 128
        s.FC = DFF // 128
        s.TOK = NT * T
        s.HALO = 128
        s.MAIN = s.TOK - s.HALO
        s.SEQ = NCORE * s.MAIN
        s.NB = T // 128
        s.GKD = GHK * 128
        s.GVD = GHV * 128
        s.GIN = 2 * s.GKD + 2 * s.GVD + 2 * GHV
        s.HVC = GHV // NCORE
        s.HKC = GHK // NCORE
        s.NCH = s.SEQ // 128


V_MIX0, V_FFN0, V_MIX1, V_FFN1, V_MIX2, V_FFN2, V_MIX3, V_FFN3, V_FINAL = range(9)
V_CB1V, V_CB1G, V_CBDW, V_CLNG, V_CLNB, V_CB2 = range(9, 15)
V_DCONV = 15
V_CDW = 18
def NVEC(cfg): return 18 + cfg.CK


class KB:
    def __init__(s, cfg, name, parent=None):
        s.cfg = cfg
        s.parent = parent
        s.fz = parent is not None
        s.over = {}
        if parent is None:
            s.nc = bass.Bass("TRN2", target_bir_lowering=False)
            s.reg = {}
            s.P = None
        else:
            s.nc, s.reg, s.P = parent.nc, parent.reg, parent.P
        s.es = ExitStack()

    def din(s, name, shape, dt=F32):
        if name in s.over:
            return s.over[name]
        if name not in s.reg:
            s.reg[name] = s.nc.dram_tensor(name, list(shape), dt, kind="ExternalInput").ap()
        return s.reg[name]

    def dw(s, name, K_, N_):
        return s.din(name, [N_ // 128, 128, K_ // 128, 128])

    def dout(s, name, shape, dt=F32):
        if name in s.over:
            return s.over[name]
        if name not in s.reg:
            s.reg[name] = s.nc.dram_tensor(name, list(shape), dt, kind="ExternalOutput").ap()
        return s.reg[name]

    def setup(s, wb_cols=384, g_bytes=None, need_xh=True):
        cfg, nc = s.cfg, s.nc
        if s.P is None:
            s.P = Prog(nc, s.es)
        P = s.P
        P.es = s.es
        P.prefix = type(s).__name__ + "_" if s.fz else ""
        T, KC = cfg.T, cfg.KC
        if need_xh:
            s.X = Buf(P, "X", [128, KC, T], F32, KC)
            s.H = Buf(P, "H", [128, KC, T], BF16, KC)
            s.NWB = 3
            s.WBC = wb_cols
            s.WB = [Buf(P, "wb%d" % i, [128, 8 * wb_cols], BF16) for i in range(s.NWB)]
            for b in s.WB:
                b.dsem = P.new_dma_sem()
                b.nat = b.t[:, :].rearrange("p (k n) -> p k n", k=8)
                b.til = b.t[:, :].rearrange("p (o e) -> p o e", o=wb_cols // 128)
            s.wb_i = 0
        if s.fz:
            s.PS, s.PSR = s.parent.PS, s.parent.PSR
        else:
            s.PS = [P.ps("ps%d" % i, [128, 512]) for i in range(8)]
            s.PSR = [Res(excl=True) for _ in range(8)]
        s.bset = 0
        s.misc_i = 0
        cst = s.din("consts", [128, 4, 128])
        s.CF = Buf(P, "cf", [128, 4, 128], F32)
        s.CB = Buf(P, "cb", [128, 4, 128], BF16)
        s.ld = P.new_dma_sem()
        P.dma("sp", s.ld, [], [s.CF.res[0]], s.CF.t[:], cst)
        P.op("dve", [s.CF.res[0]], [s.CB.res[0]], lambda: nc.vector.tensor_copy(out=s.CB.t[:], in_=s.CF.t[:]))
        s.ones_b = s.CB.t[:, 0, :]
        s.ones_f = s.CF.t[:, 0, :]
        s.ident_f = s.CF.t[:, 1, :]
        s.cres = [s.CF.res[0], s.CB.res[0]]
        s.iosem = [P.new_dma_sem() for _ in range(4)]
        if not need_xh:
            return
        s.sq = [Buf(P, "sq%d" % i, [128, T], BF16) for i in range(2)]
        s.sq_i = 0
        s.rstd = Buf(P, "rstd", [128, T], F32)
        s.tmpf = [Buf(P, "tmpf%d" % i, [128, T + 32], F32) for i in range(12)]
        s.tmp_i = 0
        for b in s.tmpf:
            b.dsem = P.new_dma_sem()

    def tmp(s):
        b = s.tmpf[s.tmp_i % len(s.tmpf)]
        s.tmp_i += 1
        return b

    def misc_bank(s):
        i = 6 + (s.misc_i % 2)
        s.misc_i += 1
        return i

    def gemm(s, W, Kc, in_ap, in_res, ocs, evac, Tw, cw=128, k0=0):
        P, nc = s.P, s.nc
        tiled = len(W.shape) == 4
        if not tiled:
            Wv = W.rearrange("(kc p) n -> p kc n", p=128)
        nbw = s.WBC // 128
        i = 0
        while i < len(ocs):
            blk = [ocs[i]]
            while len(blk) < nbw and i + len(blk) < len(ocs) and ocs[i + len(blk)] == blk[-1] + 1:
                blk.append(ocs[i + len(blk)])
            i += len(blk)
            nj = len(blk)
            banks = [s.bset * 3 + j for j in range(nj)]
            s.bset ^= 1
            n0 = blk[0] * cw
            ncols = nj * cw
            for kg in range(0, Kc, 8):
                kn = min(8, Kc - kg)
                wb = s.WB[s.wb_i % s.NWB]
                s.wb_i += 1
                if tiled:
                    src = W[blk[0]:blk[0] + nj, :, k0 + kg:k0 + kg + kn, :].rearrange("o p k c -> p o (k c)")
                    P.dma("pool", wb.dsem, [], [wb.res[0]], wb.til[:, 0:nj, 0:kn * 128], src)
                else:
                    P.dma("pool", wb.dsem, [], [wb.res[0]], wb.nat[:, 0:kn, 0:ncols], Wv[:, k0 + kg:k0 + kg + kn, n0:n0 + ncols])
                for kl in range(kn):
                    kc = kg + kl
                    for j, bk in enumerate(banks):
                        lhs = wb.til[:, j, kl * 128:(kl + 1) * 128] if tiled else wb.nat[:, kl, j * cw:(j + 1) * cw]
                        P.op("pe", [wb.res[0], in_res(kc)], [s.PSR[bk]],
                             lambda bk=bk, kc=kc, lhs=lhs: nc.tensor.matmul(
                                 s.PS[bk][0:cw, 0:Tw], lhsT=lhs, rhs=in_ap(kc),
                                 start=(kc == 0), stop=(kc == Kc - 1)))
            for j, bk in enumerate(banks):
                evac(blk[j], s.PS[bk][0:cw, 0:Tw], s.PSR[bk])

    def rmsnorm(s, gvec, Tw, out_f32=None):
        P, nc, cfg = s.P, s.nc, s.cfg
        KC = cfg.KC
        bk = s.misc_bank()
        for kc in range(KC):
            sq = s.sq[s.sq_i % 2]
            s.sq_i += 1
            P.op("act", [s.X.res[kc]], [sq.res[0]],
                 lambda kc=kc, sq=sq: nc.scalar.activation(out=sq.t[:, 0:Tw], in_=s.X.t[:, kc, 0:Tw], func=AF.Square))
            P.op("pe", [sq.res[0]] + s.cres, [s.PSR[bk]],
                 lambda kc=kc, sq=sq: nc.tensor.matmul(s.PS[bk][:, 0:Tw], lhsT=s.ones_b, rhs=sq.t[:, 0:Tw],
                                                       start=(kc == 0), stop=(kc == KC - 1)))
        P.op("act", [s.PSR[bk]], [s.rstd.res[0]],
             lambda: nc.scalar.activation(out=s.rstd.t[:, 0:Tw], in_=s.PS[bk][:, 0:Tw], func=AF.Sqrt,
                                          scale=1.0 / cfg.D, bias=s.eps6[:, 0:1]))
        P.op("dve", [s.rstd.res[0]], [s.rstd.res[0]],
             lambda: nc.vector.reciprocal(out=s.rstd.t[:, 0:Tw], in_=s.rstd.t[:, 0:Tw]))
        for kc in range(KC):
            if out_f32 is None:
                oap, ores = s.H.t[:, kc, 0:Tw], s.H.res[kc]
            else:
                oap, ores = out_f32(kc)
            P.op("dve", [s.X.res[kc], s.rstd.res[0], s.VEC.res[0]], ores if isinstance(ores, list) else [ores],
                 lambda kc=kc, oap=oap: nc.vector.scalar_tensor_tensor(
                     out=oap, in0=s.X.t[:, kc, 0:Tw], scalar=s.VEC.t[:, gvec, kc:kc + 1],
                     in1=s.rstd.t[:, 0:Tw], op0=ALU.mult, op1=ALU.mult))

    def load_vecs(s, nvec, nfl):
        P, cfg = s.P, s.cfg
        s.VEC = Buf(P, "vec", [128, nvec, cfg.KC], F32)
        P.dma("sp", s.ld, [], [s.VEC.res[0]], s.VEC.t[:], s.din("vecs", [128, nvec, cfg.KC]))
        s.FVEC = Buf(P, "fvec", [128, nfl * 4, cfg.FC], F32)
        P.dma("sp", s.ld, [], [s.FVEC.res[0]], s.FVEC.t[:], s.din("fvecs_%d" % nfl, [128, nfl * 4, cfg.FC]))
        s.HV = Buf(P, "hv", [128, 1], F32)
        P.dma("sp", s.ld, [], [s.HV.res[0]], s.HV.t[:], s.din("hv_in", [128, 1]))
        s.EPS = Buf(P, "eps", [128, 4], F32)
        P.op("dve", [], [s.EPS.res[0]], lambda: s.nc.vector.memset(s.EPS.t[:, 2:3], 1.0))
        s.one1 = s.EPS.t[:, 2:3]
        P.op("dve", [], [s.EPS.res[0]], lambda: s.nc.vector.memset(s.EPS.t[:, 0:1], 1e-6))
        P.op("dve", [], [s.EPS.res[0]], lambda: s.nc.vector.memset(s.EPS.t[:, 1:2], 1e-5))
        s.eps6 = s.EPS.t[:, 0:1]
        s.eps5 = s.EPS.t[:, 1:2]
        s.cres = s.cres + [s.EPS.res[0]]

    def dwconv(s, src, src_res, Kw, wcol, bias_ap, out_ap, out_res, Tw, extra_res=()):
        P, nc = s.P, s.nc
        rd = [src_res] + list(extra_res)
        if not isinstance(out_res, (list, tuple)):
            out_res = [out_res]
        out_res = list(out_res)
        k = Kw - 1
        if bias_ap is not None:
            P.op("dve", rd, out_res, lambda: nc.vector.tensor_scalar(
                out=out_ap, in0=src[:, k:k + Tw], scalar1=wcol(k), scalar2=bias_ap, op0=ALU.mult, op1=ALU.add))
        else:
            P.op("dve", rd, out_res, lambda: nc.vector.tensor_scalar(
                out=out_ap, in0=src[:, k:k + Tw], scalar1=wcol(k), scalar2=None, op0=ALU.mult))
        for k in range(Kw - 2, -1, -1):
            P.op("dve", rd + out_res, out_res, lambda k=k: nc.vector.scalar_tensor_tensor(
                out=out_ap, in0=src[:, k:k + Tw], scalar=wcol(k), in1=out_ap, op0=ALU.mult, op1=ALU.add))

    def ffn(s, li, fl, gvec, Wg, Wu, Wd, ti, Tw):
        P, nc, cfg = s.P, s.nc, s.cfg
        KC, FC = cfg.KC, cfg.FC
        s.rmsnorm(gvec, Tw)
        half = (FC + 1) // 2
        car = s.fcar[li]
        for h0 in range(0, FC, half):
            fcs = list(range(h0, min(FC, h0 + half)))
            gb = {}

            def evac_gate(fc, ps, psr):
                b = s.tmp()
                gb[fc] = b
                if ti == 0:
                    P.op("dve", [], [b.res[0]], lambda: nc.vector.memset(b.t[:, 0:2], 0.0))
                else:
                    P.op("dve", [car.res[fc]], [b.res[0]],
                         lambda: nc.vector.tensor_copy(out=b.t[:, 0:2], in_=car.t[:, fc, :]))
                P.op("act", [psr], [b.res[0]], lambda: nc.scalar.copy(out=b.t[:, 2:2 + Tw], in_=ps))
                if ti == 0:
                    P.op("dve", [b.res[0], s.HV.res[0]], [b.res[0]], lambda: nc.vector.tensor_scalar(
                        out=b.t[:, 2:2 + 128], in0=b.t[:, 2:2 + 128], scalar1=s.HV.t[:, 0:1], scalar2=None, op0=ALU.mult))
                P.op("dve", [b.res[0]], [car.res[fc]],
                     lambda: nc.vector.tensor_copy(out=car.t[:, fc, :], in_=b.t[:, Tw:Tw + 2]))

            def evac_up(fc, ps, psr):
                b = gb.pop(fc)
                c = s.tmp()
                s.dwconv(b.t, b.res[0], 3, lambda k: s.FVEC.t[:, fl * 4 + k, fc:fc + 1],
                         s.FVEC.t[:, fl * 4 + 3, fc:fc + 1], c.t[:, 0:Tw], c.res[0], Tw, [s.FVEC.res[0]])
                P.op("act", [c.res[0]], [c.res[0]],
                     lambda: nc.scalar.activation(out=c.t[:, 0:Tw], in_=c.t[:, 0:Tw], func=AF.Silu))
                P.op("dve", [c.res[0], psr], [s.G.res[fc - h0]], lambda: nc.vector.tensor_tensor(
                    out=s.Gb[:, fc - h0, 0:Tw], in0=c.t[:, 0:Tw], in1=ps, op=ALU.mult))

            nbw = s.WBC // 128
            for i in range(0, len(fcs), nbw):
                blk = fcs[i:i + nbw]
                s.gemm(Wg, KC, lambda kc: s.H.t[:, kc, 0:Tw], lambda kc: s.H.res[kc], blk, evac_gate, Tw)
                s.gemm(Wu, KC, lambda kc: s.H.t[:, kc, 0:Tw], lambda kc: s.H.res[kc], blk, evac_up, Tw)

            def evac_down(oc, ps, psr):
                P.op("dve", [s.X.res[oc], psr], [s.X.res[oc]], lambda: nc.vector.tensor_tensor(
                    out=s.X.t[:, oc, 0:Tw], in0=s.X.t[:, oc, 0:Tw], in1=ps, op=ALU.add))

            s.gemm(Wd, len(fcs), lambda kc: s.Gb[:, kc, 0:Tw], lambda kc: s.G.res[kc], list(range(KC)), evac_down, Tw, k0=h0)

    def alloc_g(s):
        P, cfg = s.P, s.cfg
        T, KC = cfg.T, cfg.KC
        s.G = Buf(P, "G", [128, KC * T], F32, 2 * KC)
        s.Gf = s.G.t[:].rearrange("p (n t) -> p n t", t=T)
        s.Gb = s.G.t[:].bitcast(BF16).rearrange("p (n t) -> p n t", t=T)

    def gfres(s, k):
        return [s.G.res[2 * k], s.G.res[2 * k + 1]]

    def finish(s, out_res):
        s.P.wait_all("sp", out_res)


class StageC(KB):
    def build(s):
        cfg, nc = s.cfg, s.nc
        D, KC, T, TOK, MAIN = cfg.D, cfg.KC, cfg.T, cfg.TOK, cfg.MAIN
        s.setup()
        P = s.P
        s.alloc_g()
        s.load_vecs(NVEC(cfg), 3)
        x2T = s.din("x2T", [D, TOK])
        zsT = s.din("zsT", [D, TOK])
        if not s.fz:
            onT = s.din("onT", [D, TOK])
            onv = onT.rearrange("(kc p) t -> p kc t", p=128)
        else:
            cid = s.parent.cid
            NCR, HVC = cfg.NCORE, cfg.HVC
            bigv = s.over["big"].rearrange("b (r q) c -> b r q c", r=NCR)
            mybig = s.over["mybig"]
            myres = Res()
            P.dma("sp", s.iosem[1], [], [myres], mybig[:, 0:128, :].rearrange("r q c -> r (q c)"),
                  bigv[bass.ds(cid, 1), :, cfg.MAIN - 128:cfg.MAIN, :].rearrange("o r q c -> (o r) (q c)"))
            P.dma("sp", s.iosem[2], [], [myres], mybig[:, 128:cfg.TOK, :].rearrange("r q c -> r (q c)"),
                  bigv[bass.ds(cid + 1, 1), :, :, :].rearrange("o r q c -> (o r) (q c)"))
        W = {}
        for nm, shp in [("b_w_o", [D, D]), ("c_w_pw1", [D, 2 * D]), ("c_w_pw2", [D, D]), ("d_w_in", [D, 3 * D]),
                        ("d_w_out", [D, D])]:
            W[nm] = s.dw(nm, shp[0], shp[1])
        for l in (1, 2, 3):
            W["wg%d" % l] = s.dw("wg%d" % l, D, cfg.DFF)
            W["wu%d" % l] = s.dw("wu%d" % l, D, cfg.DFF)
            W["wd%d" % l] = s.dw("wd%d" % l, cfg.DFF, D)
        outT = s.dout("outT", [D, MAIN])
        s.fcar = [Buf(P, "fcar%d" % i, [128, cfg.FC, 2], F32, cfg.FC) for i in range(3)]
        s.ccar = Buf(P, "ccar", [128, KC, cfg.CK - 1], F32, KC)
        s.dcar = Buf(P, "dcar", [128, KC, 2], F32, KC)
        s.ubuf = [Buf(P, "ubuf%d" % i, [128, cfg.CK - 1 + T], F32) for i in range(2)]
        s.u_i = 0
        s.stat = Buf(P, "stat", [128, 2, T], F32, 2)
        x2v = x2T.rearrange("(kc p) t -> p kc t", p=128)
        zsv = zsT.rearrange("(kc p) t -> p kc t", p=128)
        outv = outT.rearrange("(kc p) t -> p kc t", p=128)
        out_res = []
        for ti in range(cfg.NT):
            t0 = ti * T
            Tw = T
            for kc in range(0, KC, 8):
                ke = min(KC, kc + 8)
                P.dma("sp", s.iosem[0], [], s.X.res[kc:ke], s.X.t[:, kc:ke, :], x2v[:, kc:ke, t0:t0 + T])
            for kc in range(KC):
                a, b = s.tmp(), s.tmp()
                P.dma("sp", b.dsem, [], [b.res[0]], b.t[:, 0:T], zsv[:, kc, t0:t0 + T])
                if not s.fz:
                    P.dma("sp", a.dsem, [], [a.res[0]], a.t[:, 0:T], onv[:, kc, t0:t0 + T])
                    P.op("dve", [a.res[0], b.res[0]], [s.H.res[kc]], lambda a=a, b=b, kc=kc: nc.vector.tensor_tensor(
                        out=s.H.t[:, kc, :], in0=a.t[:, 0:T], in1=b.t[:, 0:T], op=ALU.mult))
                    continue
                r, hv = kc // HVC, kc % HVC
                mb = s.misc_bank()
                for bb in range(cfg.NB):
                    bl = ti * cfg.NB + bb
                    src = mybig[r, bl * 128:(bl + 1) * 128, hv * 128:(hv + 1) * 128]
                    P.dma("sp", a.dsem, [myres], [a.res[0]], a.t[:, bb * 128:(bb + 1) * 128], src)
                for bb in range(cfg.NB):
                    P.op("pe", [a.res[0]] + s.cres, [s.PSR[mb]], lambda bb=bb: nc.tensor.transpose(
                        out=s.PS[mb][:, bb * 128:(bb + 1) * 128], in_=a.t[:, bb * 128:(bb + 1) * 128], identity=s.ident_f))
                P.op("dve", [s.PSR[mb], b.res[0]], [s.H.res[kc]], lambda b=b, kc=kc: nc.vector.tensor_tensor(
                    out=s.H.t[:, kc, :], in0=b.t[:, 0:T], in1=s.PS[mb][:, 0:T], op=ALU.mult))
            s.gemm(W["b_w_o"], KC, lambda kc: s.H.t[:, kc, 0:Tw], lambda kc: s.H.res[kc], list(range(KC)), s.evac_resid(Tw), Tw)
            s.ffn(0, 0, V_FFN1, W["wg1"], W["wu1"], W["wd1"], ti, Tw)
            s.conformer(W["c_w_pw1"], W["c_w_pw2"], ti, Tw)
            s.ffn(1, 1, V_FFN2, W["wg2"], W["wu2"], W["wd2"], ti, Tw)
            s.shortconv(W["d_w_in"], W["d_w_out"], ti, Tw)
            s.ffn(2, 2, V_FFN3, W["wg3"], W["wu3"], W["wd3"], ti, Tw)
            s.rmsnorm(V_FINAL, Tw, out_f32=lambda kc: (s.Gf[:, kc, 0:Tw], s.gfres(kc)))
            c0 = cfg.HALO if ti == 0 else 0
            o0 = t0 + c0 - cfg.HALO
            for kc in range(0, KC, 8):
                ke = min(KC, kc + 8)
                rr = [r for k in range(kc, ke) for r in s.gfres(k)]
                dres = Res()
                P.dma("sp", s.iosem[3], rr, [dres], outv[:, kc:ke, o0:o0 + T - c0], s.Gf[:, kc:ke, c0:T])
                out_res.append(dres)
        s.finish(out_res)
        return s

    def evac_resid(s, Tw):
        P, nc = s.P, s.nc

        def ev(oc, ps, psr):
            P.op("dve", [s.X.res[oc], psr], [s.X.res[oc]], lambda: nc.vector.tensor_tensor(
                out=s.X.t[:, oc, 0:Tw], in0=s.X.t[:, oc, 0:Tw], in1=ps, op=ALU.add))
        return ev

    def conformer(s, W1, W2, ti, Tw):
        P, nc, cfg = s.P, s.nc, s.cfg
        KC, D, CK = cfg.KC, cfg.D, cfg.CK
        HK = CK - 1
        s.rmsnorm(V_MIX2, Tw)
        sg = {}

        def evac_gate(oc, ps, psr):
            c = oc - KC
            b = s.tmp()
            sg[c] = b
            P.op("act", [psr, s.VEC.res[0]], [b.res[0]], lambda: nc.scalar.activation(
                out=b.t[:, 0:Tw], in_=ps, func=AF.Sigmoid, bias=s.VEC.t[:, V_CB1G, c:c + 1], scale=1.0))

        def evac_val(c, ps, psr):
            b = sg.pop(c)
            u = s.ubuf[s.u_i % 2]
            s.u_i += 1
            if ti == 0:
                P.op("dve", [], [u.res[0]], lambda: nc.vector.memset(u.t[:, 0:HK], 0.0))
            else:
                P.op("dve", [s.ccar.res[c]], [u.res[0]], lambda: nc.vector.tensor_copy(out=u.t[:, 0:HK], in_=s.ccar.t[:, c, :]))
            P.op("dve", [psr, b.res[0], s.VEC.res[0]], [u.res[0]], lambda: nc.vector.scalar_tensor_tensor(
                out=u.t[:, HK:HK + Tw], in0=ps, scalar=s.VEC.t[:, V_CB1V, c:c + 1], in1=b.t[:, 0:Tw],
                op0=ALU.add, op1=ALU.mult))
            if ti == 0:
                P.op("dve", [u.res[0], s.HV.res[0]], [u.res[0]], lambda: nc.vector.tensor_scalar(
                    out=u.t[:, HK:HK + 128], in0=u.t[:, HK:HK + 128], scalar1=s.HV.t[:, 0:1], scalar2=None, op0=ALU.mult))
            P.op("dve", [u.res[0]], [s.ccar.res[c]], lambda: nc.vector.tensor_copy(out=s.ccar.t[:, c, :], in_=u.t[:, Tw:Tw + HK]))
            s.dwconv(u.t, u.res[0], CK, lambda k: s.VEC.t[:, V_CDW + k, c:c + 1], s.VEC.t[:, V_CBDW, c:c + 1],
                     s.Gf[:, c, 0:Tw], s.gfres(c), Tw, [s.VEC.res[0]])

        nbw = s.WBC // 128
        for i in range(0, KC, nbw):
            blk = list(range(i, min(KC, i + nbw)))
            s.gemm(W1, KC, lambda kc: s.H.t[:, kc, 0:Tw], lambda kc: s.H.res[kc], [KC + c for c in blk], evac_gate, Tw)
            s.gemm(W1, KC, lambda kc: s.H.t[:, kc, 0:Tw], lambda kc: s.H.res[kc], blk, evac_val, Tw)
        b1, b2 = s.misc_bank(), s.misc_bank()
        for c in range(KC):
            P.op("pe", s.gfres(c) + s.cres, [s.PSR[b1]], lambda c=c: nc.tensor.matmul(
                s.PS[b1][:, 0:Tw], lhsT=s.ones_f, rhs=s.Gf[:, c, 0:Tw], start=(c == 0), stop=(c == KC - 1)))
        for c in range(KC):
            q = s.tmp()
            P.op("act", s.gfres(c), [q.res[0]], lambda c=c, q=q: nc.scalar.activation(
                out=q.t[:, 0:Tw], in_=s.Gf[:, c, 0:Tw], func=AF.Square))
            P.op("pe", [q.res[0]] + s.cres, [s.PSR[b2]], lambda c=c, q=q: nc.tensor.matmul(
                s.PS[b2][:, 0:Tw], lhsT=s.ones_f, rhs=q.t[:, 0:Tw], start=(c == 0), stop=(c == KC - 1)))
        mean, rs = s.stat.t[:, 0, 0:Tw], s.stat.t[:, 1, 0:Tw]
        sr = s.stat.res
        P.op("act", [s.PSR[b1]], [sr[0]], lambda: nc.scalar.activation(out=mean, in_=s.PS[b1][:, 0:Tw], func=AF.Copy, scale=1.0 / D))
        q = s.tmp()
        P.op("dve", [sr[0]], [q.res[0]], lambda: nc.vector.tensor_tensor(out=q.t[:, 0:Tw], in0=mean, in1=mean, op=ALU.mult))
        P.op("dve", [s.PSR[b2], q.res[0]], [sr[1]], lambda: nc.vector.scalar_tensor_tensor(
            out=rs, in0=s.PS[b2][:, 0:Tw], scalar=1.0 / D, in1=q.t[:, 0:Tw], op0=ALU.mult, op1=ALU.subtract))
        P.op("act", [sr[1]] + s.cres, [sr[1]], lambda: nc.scalar.activation(out=rs, in_=rs, func=AF.Sqrt, bias=s.eps5, scale=1.0))
        P.op("dve", [sr[1]], [sr[1]], lambda: nc.vector.reciprocal(out=rs, in_=rs))
        for c in range(KC):
            q = s.tmp()
            P.op("dve", s.gfres(c) + [sr[0]], [q.res[0]], lambda c=c, q=q: nc.vector.tensor_tensor(
                out=q.t[:, 0:Tw], in0=s.Gf[:, c, 0:Tw], in1=mean, op=ALU.subtract))
            P.op("dve", [q.res[0], sr[1]], [q.res[0]], lambda q=q: nc.vector.tensor_tensor(
                out=q.t[:, 0:Tw], in0=q.t[:, 0:Tw], in1=rs, op=ALU.mult))
            P.op("act", [q.res[0], s.VEC.res[0]], [s.H.res[c]], lambda c=c, q=q: nc.scalar.activation(
                out=s.H.t[:, c, 0:Tw], in_=q.t[:, 0:Tw], func=AF.Silu, scale=s.VEC.t[:, V_CLNG, c:c + 1],
                bias=s.VEC.t[:, V_CLNB, c:c + 1]))

        def evac2(oc, ps, psr):
            P.op("dve", [s.X.res[oc], psr, s.VEC.res[0]], [s.X.res[oc]], lambda: nc.vector.scalar_tensor_tensor(
                out=s.X.t[:, oc, 0:Tw], in0=ps, scalar=s.VEC.t[:, V_CB2, oc:oc + 1], in1=s.X.t[:, oc, 0:Tw],
                op0=ALU.add, op1=ALU.add))
        s.gemm(W2, KC, lambda kc: s.H.t[:, kc, 0:Tw], lambda kc: s.H.res[kc], list(range(KC)), evac2, Tw)

    def shortconv(s, Win, Wout, ti, Tw):
        P, nc, cfg = s.P, s.nc, s.cfg
        KC = cfg.KC
        s.rmsnorm(V_MIX3, Tw)
        cgb, cvb = {}, {}
        hin = (lambda kc: s.H.t[:, kc, 0:Tw]), (lambda kc: s.H.res[kc])

        def evac_cg(oc, ps, psr):
            b = s.tmp()
            cgb[oc - KC] = b
            P.op("act", [psr], [b.res[0]], lambda: nc.scalar.copy(out=b.t[:, 0:Tw], in_=ps))

        def evac_xin(oc, ps, psr):
            c = oc - 2 * KC
            b = cgb.pop(c)
            u = s.tmp()
            if ti == 0:
                P.op("dve", [], [u.res[0]], lambda: nc.vector.memset(u.t[:, 0:2], 0.0))
            else:
                P.op("dve", [s.dcar.res[c]], [u.res[0]], lambda: nc.vector.tensor_copy(out=u.t[:, 0:2], in_=s.dcar.t[:, c, :]))
            P.op("dve", [psr, b.res[0]], [u.res[0]], lambda: nc.vector.tensor_tensor(
                out=u.t[:, 2:2 + Tw], in0=b.t[:, 0:Tw], in1=ps, op=ALU.mult))
            if ti == 0:
                P.op("dve", [u.res[0], s.HV.res[0]], [u.res[0]], lambda: nc.vector.tensor_scalar(
                    out=u.t[:, 2:2 + 128], in0=u.t[:, 2:2 + 128], scalar1=s.HV.t[:, 0:1], scalar2=None, op0=ALU.mult))
            P.op("dve", [u.res[0]], [s.dcar.res[c]], lambda: nc.vector.tensor_copy(out=s.dcar.t[:, c, :], in_=u.t[:, Tw:Tw + 2]))
            v = s.tmp()
            cvb[c] = v
            s.dwconv(u.t, u.res[0], 3, lambda k: s.VEC.t[:, V_DCONV + k, c:c + 1], None, v.t[:, 0:Tw], v.res[0], Tw, [s.VEC.res[0]])

        def evac_bg(c, ps, psr):
            v = cvb.pop(c)
            P.op("dve", [psr, v.res[0]], [s.G.res[c]], lambda: nc.vector.tensor_tensor(
                out=s.Gb[:, c, 0:Tw], in0=v.t[:, 0:Tw], in1=ps, op=ALU.mult))

        nbw = s.WBC // 128
        for i in range(0, KC, nbw):
            blk = list(range(i, min(KC, i + nbw)))
            s.gemm(Win, KC, hin[0], hin[1], [KC + c for c in blk], evac_cg, Tw)
            s.gemm(Win, KC, hin[0], hin[1], [2 * KC + c for c in blk], evac_xin, Tw)
            s.gemm(Win, KC, hin[0], hin[1], blk, evac_bg, Tw)
        s.gemm(Wout, KC, lambda kc: s.Gb[:, kc, 0:Tw], lambda kc: s.G.res[kc], list(range(KC)), s.evac_resid(Tw), Tw)


class StageA(KB):
    def build(s):
        cfg, nc = s.cfg, s.nc
        D, KC, T, TOK, MAIN, NB = cfg.D, cfg.KC, cfg.T, cfg.TOK, cfg.MAIN, cfg.NB
        AH, AKV = cfg.AH, cfg.AKV
        s.setup()
        P = s.P
        s.alloc_g()
        s.load_vecs(NVEC(cfg), 1)
        xT = s.din("xT", [D, 128 + TOK])
        Wqkv = s.din("a_w_qkv", [D, (AH + 2 * AKV) * 128])
        Wo = s.din("a_w_o", [D, D])
        Wg, Wu, Wd = s.din("wg0", [D, cfg.DFF]), s.din("wu0", [D, cfg.DFF]), s.din("wd0", [cfg.DFF, D])
        Win = s.din("b_w_in", [D, cfg.GIN])
        if s.fz:
            x2T, zsT = s.over["x2s"], s.over["zss"]
            qkT = vT = gbT = None
        else:
            x2T = s.dout("x2T", [D, MAIN])
            qkT = s.dout("qkT", [2 * cfg.GKD, MAIN])
            vT = s.dout("vT", [cfg.GVD, MAIN])
            zsT = s.dout("zsT", [cfg.GVD, MAIN])
            gbT = s.dout("gbT", [2 * cfg.GHV, MAIN])
        s.fcar = [Buf(P, "fcar0", [128, cfg.FC, 2], F32, cfg.FC)]
        s.TAB = Buf(P, "tab", [128, 2, 128 + TOK], F32)
        P.dma("sp", s.ld, [], [s.TAB.res[0]], s.TAB.t[:], s.din("rope_tab", [128, 2, 128 + TOK]))
        s.MK = Buf(P, "mk", [128, 2, 256], F32)
        P.dma("sp", s.ld, [], [s.MK.res[0]], s.MK.t[:], s.din("masks", [128, 2, 256]))
        s.SK = Buf(P, "sk", [128, AH], F32)
        P.dma("sp", s.ld, [], [s.SK.res[0]], s.SK.t[:], s.din("sinkb", [128, AH]))
        NQK = 2 * cfg.GHK + cfg.GHV
        s.GCV = Buf(P, "gcv", [128, 4, NQK], F32)
        P.dma("sp", s.ld, [], [s.GCV.res[0]], s.GCV.t[:], s.din("gconv", [128, 4, NQK]))
        s.GS = Buf(P, "gs", [cfg.GHV, 2], F32)
        P.dma("sp", s.ld, [], [s.GS.res[0]], s.GS.t[:], s.din("gsc", [cfg.GHV, 2]))
        P.op("act", [s.GS.res[0]], [s.GS.res[0]], lambda: nc.scalar.activation(out=s.GS.t[:, 1:2], in_=s.GS.t[:, 1:2], func=AF.Exp))
        P.op("dve", [s.GS.res[0]], [s.GS.res[0]], lambda: nc.vector.tensor_scalar(
            out=s.GS.t[:, 1:2], in0=s.GS.t[:, 1:2], scalar1=-1.0, scalar2=None, op0=ALU.mult))
        s.KT = Buf(P, "KT", [128, AKV, 128 + T], BF16, AKV)
        s.V = Buf(P, "V", [128, 1 + NB, AKV * 128], BF16, 1 + NB)
        s.gcar = Buf(P, "gcar", [128, NQK, 3], F32, NQK)
        s.st = [Buf(P, "st%d" % i, [128, 8], F32) for i in range(4)]
        s.pts = [Buf(P, "pts%d" % i, [128, 256], BF16) for i in range(2)]
        s.u_i = 0
        xv = xT.rearrange("(kc p) t -> p kc t", p=128)
        x2v = x2T.rearrange("(kc p) t -> p kc t", p=128)
        out_res = []
        hin = (lambda Tw: (lambda kc: s.H.t[:, kc, 0:Tw])), (lambda kc: s.H.res[kc])

        def load_x(c0, Tw):
            for kc in range(0, KC, 8):
                ke = min(KC, kc + 8)
                P.dma("sp", s.iosem[0], [], s.X.res[kc:ke], s.X.t[:, kc:ke, 0:Tw], xv[:, kc:ke, c0:c0 + Tw])

        def rope_evac(dst, tcol0, Tw):
            def ev(oc, ps, psr):
                ap, res = dst(oc)
                qf = s.tmp()
                P.op("act", [psr], [qf.res[0]], lambda: nc.scalar.copy(out=qf.t[:, 0:Tw], in_=ps))
                mb = s.misc_bank()
                P.op("pe", [qf.res[0]] + s.cres, [s.PSR[mb]], lambda: nc.tensor.matmul(
                    s.PS[mb][:, 0:Tw], lhsT=s.CF.t[:, 2, :], rhs=qf.t[:, 0:Tw], start=True, stop=True))
                t1, t2 = s.tmp(), s.tmp()
                P.op("dve", [qf.res[0], s.TAB.res[0]], [t1.res[0]], lambda: nc.vector.tensor_tensor(
                    out=t1.t[:, 0:Tw], in0=qf.t[:, 0:Tw], in1=s.TAB.t[:, 0, tcol0:tcol0 + Tw], op=ALU.mult))
                P.op("dve", [s.PSR[mb], s.TAB.res[0]], [t2.res[0]], lambda: nc.vector.tensor_tensor(
                    out=t2.t[:, 0:Tw], in0=s.PS[mb][:, 0:Tw], in1=s.TAB.t[:, 1, tcol0:tcol0 + Tw], op=ALU.mult))
                P.op("dve", [t1.res[0], t2.res[0]], [res], lambda: nc.vector.tensor_tensor(
                    out=ap, in0=t1.t[:, 0:Tw], in1=t2.t[:, 0:Tw], op=ALU.add))
            return ev

        def vgemm(nblk, vb0):
            Wv = Wqkv.rearrange("(kc p) n -> p kc n", p=128)
            vc0 = (AH + AKV) * 128
            ntot = AKV * 128
            n0 = 0
            while n0 < ntot:
                ncols = min(s.WBC, ntot - n0)
                banks = [s.bset * 3 + j for j in range(nblk)]
                s.bset ^= 1
                for kg in range(0, KC, 8):
                    kn = min(8, KC - kg)
                    wb = s.WB[s.wb_i % s.NWB]
                    s.wb_i += 1
                    P.dma("pool", wb.dsem, [], [wb.res[0]], wb.nat[:, 0:kn, 0:ncols], Wv[:, kg:kg + kn, vc0 + n0:vc0 + n0 + ncols])
                    for kl in range(kn):
                        kc = kg + kl
                        for b, bk in enumerate(banks):
                            P.op("pe", [wb.res[0], s.H.res[kc]], [s.PSR[bk]], lambda b=b, bk=bk, kl=kl, kc=kc, wb=wb: nc.tensor.matmul(
                                s.PS[bk][:, 0:ncols], lhsT=s.H.t[:, kc, b * 128:(b + 1) * 128], rhs=wb.nat[:, kl, 0:ncols],
                                start=(kc == 0), stop=(kc == KC - 1)))
                for b, bk in enumerate(banks):
                    P.op("act", [s.PSR[bk]], [s.V.res[vb0 + b]], lambda b=b, bk=bk: nc.scalar.copy(
                        out=s.V.t[:, vb0 + b, n0:n0 + ncols], in_=s.PS[bk][:, 0:ncols]))
                n0 += ncols

        load_x(0, 128)
        s.rmsnorm(V_MIX0, 128)
        s.gemm(Wqkv, KC, hin[0](128), hin[1], [AH + g for g in range(AKV)],
               rope_evac(lambda oc: (s.KT.t[:, oc - AH, 0:128], s.KT.res[oc - AH]), 0, 128), 128)
        vgemm(1, 0)

        for ti in range(cfg.NT):
            t0 = 128 + ti * T
            Tw = T
            load_x(t0, T)
            s.rmsnorm(V_MIX0, Tw)
            if s.fz and ti > 0:
                s.parent.gather_tile(s, ti - 1)
            s.gemm(Wqkv, KC, hin[0](Tw), hin[1], list(range(AH)),
                   rope_evac(lambda oc: (s.Gb[:, oc, 0:Tw], s.G.res[oc]), t0, Tw), Tw)
            s.gemm(Wqkv, KC, hin[0](Tw), hin[1], [AH + g for g in range(AKV)],
                   rope_evac(lambda oc: (s.KT.t[:, oc - AH, 128:128 + Tw], s.KT.res[oc - AH]), t0, Tw), Tw)
            vgemm(NB, 1)
            u = 0
            for b in range(NB):
                for h in range(AH):
                    s.attn_unit(ti, b, h, u)
                    u += 1
            for g in range(AKV):
                P.op("dve", [s.KT.res[g]], [s.KT.res[g]], lambda g=g: nc.vector.tensor_copy(
                    out=s.KT.t[:, g, 0:128], in_=s.KT.t[:, g, T:T + 128]))
            P.op("dve", [s.V.res[NB]], [s.V.res[0]], lambda: nc.vector.tensor_copy(out=s.V.t[:, 0, :], in_=s.V.t[:, NB, :]))
            s.gemm(Wo, KC, hin[0](Tw), hin[1], list(range(KC)), s.evac_resid(Tw), Tw)
            s.ffn(0, 0, V_FFN0, Wg, Wu, Wd, ti, Tw)
            c0 = cfg.HALO if ti == 0 else 0
            o0 = ti * T + c0 - cfg.HALO
            nm = T - c0
            if s.fz:
                c0, o0, nm = 0, ti * T, T
            for kc in range(0, KC, 8):
                ke = min(KC, kc + 8)
                dres = Res()
                P.dma("sp", s.iosem[3], s.X.res[kc:ke], [dres], x2v[:, kc:ke, o0:o0 + nm], s.X.t[:, kc:ke, c0:T])
                out_res.append(dres)
            s.rmsnorm(V_MIX1, Tw)
            s.gdn_pre(Win, qkT, vT, zsT, gbT, ti, Tw, c0, o0, nm, out_res)
        s.finish(out_res)
        return s

    def evac_resid(s, Tw):
        return StageC.evac_resid(s, Tw)

    def attn_unit(s, ti, b, h, u):
        P, nc, cfg = s.P, s.nc, s.cfg
        g = h // (cfg.AH // cfg.AKV)
        sbk = s.misc_bank()
        P.op("pe", [s.G.res[h], s.KT.res[g]], [s.PSR[sbk]], lambda: nc.tensor.matmul(
            s.PS[sbk][:, 0:256], lhsT=s.Gb[:, h, b * 128:(b + 1) * 128], rhs=s.KT.t[:, g, b * 128:b * 128 + 256],
            start=True, stop=True))
        sm = s.tmp()
        st = s.st[u % 4]
        mi = 1 if (ti == 0 and b == 1) else 0
        smr, str_ = sm.res[0], st.res[0]
        P.op("dve", [s.PSR[sbk], s.MK.res[0]], [smr], lambda: nc.vector.scalar_tensor_tensor(
            out=sm.t[:, 0:256], in0=s.PS[sbk][:, 0:256], scalar=128.0 ** -0.5, in1=s.MK.t[:, mi, :], op0=ALU.mult, op1=ALU.add))
        P.op("dve", [smr], [str_], lambda: nc.vector.reduce_max(out=st.t[:, 0:1], in_=sm.t[:, 0:256], axis=AX.X))
        P.op("dve", [str_, s.SK.res[0]], [str_], lambda: nc.vector.tensor_tensor(
            out=st.t[:, 1:2], in0=st.t[:, 0:1], in1=s.SK.t[:, h:h + 1], op=ALU.max))
        P.op("dve", [str_], [str_], lambda: nc.vector.tensor_scalar(
            out=st.t[:, 2:3], in0=st.t[:, 1:2], scalar1=-1.0, scalar2=None, op0=ALU.mult))
        P.op("act", [smr, str_], [smr, str_], lambda: nc.scalar.activation(
            out=sm.t[:, 0:256], in_=sm.t[:, 0:256], func=AF.Exp, bias=st.t[:, 2:3], scale=1.0, accum_out=st.t[:, 3:4]))
        P.op("act", [str_, s.SK.res[0]], [str_], lambda: nc.scalar.activation(
            out=st.t[:, 4:5], in_=s.SK.t[:, h:h + 1], func=AF.Exp, bias=st.t[:, 2:3], scale=1.0))
        P.op("dve", [str_], [str_], lambda: nc.vector.tensor_tensor(
            out=st.t[:, 5:6], in0=st.t[:, 3:4], in1=st.t[:, 4:5], op=ALU.add))
        P.op("dve", [str_], [str_], lambda: nc.vector.reciprocal(out=st.t[:, 6:7], in_=st.t[:, 5:6]))
        P.op("dve", [smr, str_], [smr], lambda: nc.vector.tensor_scalar(
            out=sm.t[:, 0:256], in0=sm.t[:, 0:256], scalar1=st.t[:, 6:7], scalar2=None, op0=ALU.mult))
        pb = u % 3
        ob = 3 + (u % 3)
        for j in range(2):
            P.op("pe", [smr] + s.cres, [s.PSR[pb]], lambda j=j: nc.tensor.transpose(
                out=s.PS[pb][:, j * 128:(j + 1) * 128], in_=sm.t[:, j * 128:(j + 1) * 128], identity=s.ident_f))
        pts = s.pts[u % 2]
        P.op("act", [s.PSR[pb]], [pts.res[0]], lambda: nc.scalar.copy(out=pts.t[:, :], in_=s.PS[pb][:, 0:256]))
        for j in range(2):
            P.op("pe", [pts.res[0], s.V.res[b + j]], [s.PSR[ob]], lambda j=j: nc.tensor.matmul(
                s.PS[ob][:, 0:128], lhsT=s.V.t[:, b + j, g * 128:(g + 1) * 128], rhs=pts.t[:, j * 128:(j + 1) * 128],
                start=(j == 0), stop=(j == 1)))
        P.op("act", [s.PSR[ob]], [s.H.res[h]], lambda: nc.scalar.copy(
            out=s.H.t[:, h, b * 128:(b + 1) * 128], in_=s.PS[ob][:, 0:128]))

    def gdn_pre(s, Win, qkT, vT, zsT, gbT, ti, Tw, c0, o0, nm, out_res):
        P, nc, cfg = s.P, s.nc, s.cfg
        KC = cfg.KC
        NQ2 = 2 * cfg.GHK
        NQK = NQ2 + cfg.GHV
        hin = (lambda kc: s.H.t[:, kc, 0:Tw]), (lambda kc: s.H.res[kc])
        zv_ = zsT.rearrange("(c p) t -> p c t", p=128)
        GHK_, GHV_ = cfg.GHK, cfg.GHV
        if s.fz:
            gin = s.over["gin"][ti]
            HVh = cfg.HVC // 2

            def qk_dst(oc):
                g_, i_ = (0, oc) if oc < GHK_ else (1, oc - GHK_)
                return gin[g_][i_ * 128:(i_ + 1) * 128, :]

            def v_dst(j):
                c2, hv = j // cfg.HVC, j % cfg.HVC
                r0 = (c2 * HVh + hv % HVh) * 128
                return gin[2 + hv // HVh][r0:r0 + 128, :]
            b_dst = gin[4][0:GHV_, :]
            a_dst = gin[4][GHV_:2 * GHV_, :]
        else:
            qkv_ = qkT.rearrange("(c p) t -> p c t", p=128)
            vv_ = vT.rearrange("(c p) t -> p c t", p=128)
            qk_dst = lambda oc: qkv_[:, oc, o0:o0 + nm]
            v_dst = lambda j: vv_[:, j, o0:o0 + nm]
            b_dst = gbT[0:GHV_, o0:o0 + nm]
            a_dst = gbT[GHV_:2 * GHV_, o0:o0 + nm]

        if s.fz and not hasattr(s, "grp_res"):
            s.grp_res = [[[] for _ in range(5)] for _ in range(cfg.NT)]

        def store(src, dst_ap, grp=None):
            dres = Res()
            P.dma("sp", src.dsem, [src.res[0]], [dres], dst_ap, src.t[:, c0:c0 + nm])
            out_res.append(dres)
            if s.fz and grp is not None:
                s.grp_res[ti][grp].append(dres)

        def evac_qkv(oc, ps, psr):
            cb = s.tmp()
            if ti == 0:
                P.op("dve", [], [cb.res[0]], lambda: nc.vector.memset(cb.t[:, 0:3], 0.0))
            else:
                P.op("dve", [s.gcar.res[oc]], [cb.res[0]], lambda: nc.vector.tensor_copy(out=cb.t[:, 0:3], in_=s.gcar.t[:, oc, :]))
            P.op("act", [psr], [cb.res[0]], lambda: nc.scalar.copy(out=cb.t[:, 3:3 + Tw], in_=ps))
            if ti == 0:
                P.op("dve", [cb.res[0], s.HV.res[0]], [cb.res[0]], lambda: nc.vector.tensor_scalar(
                    out=cb.t[:, 3:3 + 128], in0=cb.t[:, 3:3 + 128], scalar1=s.HV.t[:, 0:1], scalar2=None, op0=ALU.mult))
            P.op("dve", [cb.res[0]], [s.gcar.res[oc]], lambda: nc.vector.tensor_copy(out=s.gcar.t[:, oc, :], in_=cb.t[:, Tw:Tw + 3]))
            y = s.tmp()
            s.dwconv(cb.t, cb.res[0], 4, lambda k: s.GCV.t[:, k, oc:oc + 1], None, y.t[:, 0:Tw], y.res[0], Tw, [s.GCV.res[0]])
            P.op("act", [y.res[0]], [y.res[0]], lambda: nc.scalar.activation(out=y.t[:, 0:Tw], in_=y.t[:, 0:Tw], func=AF.Silu))
            if oc >= NQ2:
                store(y, v_dst(oc - NQ2), 2 + ((oc - NQ2) % cfg.HVC) // max(1, cfg.HVC // 2))
                return
            sq, rs = s.tmp(), s.tmp()
            P.op("act", [y.res[0]], [sq.res[0]], lambda: nc.scalar.activation(out=sq.t[:, 0:Tw], in_=y.t[:, 0:Tw], func=AF.Square))
            mb = s.misc_bank()
            P.op("pe", [sq.res[0]] + s.cres, [s.PSR[mb]], lambda: nc.tensor.matmul(
                s.PS[mb][:, 0:Tw], lhsT=s.ones_f, rhs=sq.t[:, 0:Tw], start=True, stop=True))
            P.op("act", [s.PSR[mb]] + s.cres, [rs.res[0]], lambda: nc.scalar.activation(
                out=rs.t[:, 0:Tw], in_=s.PS[mb][:, 0:Tw], func=AF.Sqrt, bias=s.eps6, scale=1.0))
            P.op("dve", [rs.res[0]], [rs.res[0]], lambda: nc.vector.reciprocal(out=rs.t[:, 0:Tw], in_=rs.t[:, 0:Tw]))
            P.op("dve", [y.res[0], rs.res[0]], [y.res[0]], lambda: nc.vector.tensor_tensor(
                out=y.t[:, 0:Tw], in0=y.t[:, 0:Tw], in1=rs.t[:, 0:Tw], op=ALU.mult))
            store(y, qk_dst(oc), 0 if oc < GHK_ else 1)

        def evac_z(oc, ps, psr):
            y = s.tmp()
            P.op("act", [psr], [y.res[0]], lambda: nc.scalar.activation(out=y.t[:, 0:Tw], in_=ps, func=AF.Silu))
            store(y, zv_[:, oc - NQK, o0:o0 + nm])

        s.gemm(Win, KC, hin[0], hin[1], list(range(NQK)), evac_qkv, Tw)
        s.gemm(Win, KC, hin[0], hin[1], list(range(NQK, NQK + cfg.GHV)), evac_z, Tw)

        GHV = cfg.GHV

        def evac_b(oc, ps, psr):
            o = s.tmp()
            P.op("act", [psr], [o.res[0]], lambda: nc.scalar.activation(out=o.t[0:GHV, 0:Tw], in_=ps, func=AF.Sigmoid))
            dres = Res()
            P.dma("sp", o.dsem, [o.res[0]], [dres], b_dst, o.t[0:GHV, c0:c0 + nm])
            out_res.append(dres)
            if s.fz:
                s.grp_res[ti][4].append(dres)

        def evac_a(oc, ps, psr):
            o, t, t2 = s.tmp(), s.tmp(), s.tmp()
            R_ = slice(0, GHV)
            P.op("dve", [psr, s.GS.res[0]], [t.res[0]], lambda: nc.vector.tensor_scalar(
                out=t.t[R_, 0:Tw], in0=ps, scalar1=s.GS.t[R_, 0:1], scalar2=None, op0=ALU.add))
            P.op("dve", [t.res[0]], [t2.res[0]], lambda: nc.vector.scalar_tensor_tensor(
                out=t2.t[R_, 0:Tw], in0=t.t[R_, 0:Tw], scalar=-1.0, in1=t.t[R_, 0:Tw], op0=ALU.mult, op1=ALU.min))
            P.op("act", [t2.res[0]], [t2.res[0]], lambda: nc.scalar.activation(out=t2.t[R_, 0:Tw], in_=t2.t[R_, 0:Tw], func=AF.Exp))
            P.op("act", [t2.res[0]] + s.cres, [t2.res[0]], lambda: nc.scalar.activation(
                out=t2.t[R_, 0:Tw], in_=t2.t[R_, 0:Tw], func=AF.Ln, bias=s.EPS.t[R_, 2:3], scale=1.0))
            P.op("dve", [t.res[0], t2.res[0]], [t.res[0]], lambda: nc.vector.scalar_tensor_tensor(
                out=t.t[R_, 0:Tw], in0=t.t[R_, 0:Tw], scalar=0.0, in1=t2.t[R_, 0:Tw], op0=ALU.max, op1=ALU.add))
            P.op("dve", [t.res[0], s.GS.res[0]], [o.res[0]], lambda: nc.vector.tensor_scalar(
                out=o.t[R_, 0:Tw], in0=t.t[R_, 0:Tw], scalar1=s.GS.t[R_, 1:2], scalar2=None, op0=ALU.mult))
            dres = Res()
            P.dma("sp", o.dsem, [o.res[0]], [dres], a_dst, o.t[0:GHV, c0:c0 + nm])
            out_res.append(dres)
            if s.fz:
                s.grp_res[ti][4].append(dres)

        nq = (NQK + cfg.GHV) * 128
        s.gemm(Win[:, nq:nq + GHV], KC, hin[0], hin[1], [0], evac_b, Tw, cw=GHV)
        s.gemm(Win[:, nq + GHV:nq + 2 * GHV], KC, hin[0], hin[1], [0], evac_a, Tw, cw=GHV)


NBM = 18


class StageB(KB):
    def build(s):
        cfg, nc = s.cfg, s.nc
        HVC, HKC, NCH, SEQ = cfg.HVC, cfg.HKC, cfg.NCH, cfg.SEQ
        REP = HVC // HKC
        s.setup(need_xh=False)
        P = s.P
        NCOL = NCH * HVC
        NCR, T, NBK = cfg.NCORE, cfg.T, cfg.NB
        CPR = cfg.MAIN // 128
        if s.fz:
            cid = s.parent.cid
            gout = s.over["gout"]
            HVh = HVC // 2
            mine = s.over["mine"]
            mres = Res()
            for ti in range(cfg.NT):
                for g in range(4):
                    gv = gout[ti][g].rearrange("(r c x) t -> r c (x t)", r=NCR, c=NCR)
                    mv = mine[ti][g].rearrange("(r x) t -> r (x t)", r=NCR)
                    src = gv[:, bass.ds(cid, 1), :].rearrange("r o e -> r (o e)")
                    P.dma("sp", s.iosem[(ti * 4 + g) % 4], [], [mres], mv, src)
            g4 = [gout[ti][4].rearrange("(r y) t -> r y t", r=NCR) for ti in range(cfg.NT)]
            XQ, XV = HKC * 128, HVh * 128

            def loc(n):
                r, bl = n // CPR, n % CPR + 1
                return r, bl // NBK, (bl % NBK) * 128

            def q_src(n, hk, which):
                r, ti, c0_ = loc(n)
                return mine[ti][which][r * XQ + hk * 128:r * XQ + (hk + 1) * 128, c0_:c0_ + 128]

            def v_src(n, hv):
                r, ti, c0_ = loc(n)
                hh = hv % HVh
                return mine[ti][2 + hv // HVh][r * XV + hh * 128:r * XV + (hh + 1) * 128, c0_:c0_ + 128]
            o_loc = s.over["o_loc"]

            def o_dst(n, hv):
                return o_loc[n // CPR][(n % CPR) * 128:(n % CPR + 1) * 128, hv * 128:(hv + 1) * 128]
        else:
            qT = s.din("qT", [HKC * 128, SEQ])
            kT = s.din("kT", [HKC * 128, SEQ])
            vT = s.din("vT", [HVC * 128, SEQ])
            gtm = s.din("gtm", [128, NCOL])
            btm = s.din("btm", [128, NCOL])
            o_tm = s.dout("o_tm", [SEQ, HVC * 128])
            mres = Res()
            q_src = lambda n, hk, which: (qT, kT)[which][hk * 128:(hk + 1) * 128, n * 128:(n + 1) * 128]
            v_src = lambda n, hv: vT[hv * 128:(hv + 1) * 128, n * 128:(n + 1) * 128]
            o_dst = lambda n, hv: o_tm[n * 128:(n + 1) * 128, hv * 128:(hv + 1) * 128]
        s.BM = Buf(P, "bm", [128, NBM, 128], F32)
        P.dma("sp", s.ld, [], [s.BM.res[0]], s.BM.t[:], s.din("bmasks", [128, NBM, 128]))
        s.EPS = Buf(P, "eps", [128, 2], F32)
        P.op("dve", [], [s.EPS.res[0]], lambda: nc.vector.memset(s.EPS.t[:, 0:1], 1e-6))
        bm = lambda i: s.BM.t[:, i, :]
        BMR = s.BM.res[0]
        SC = Buf(P, "sc", [128, 8, NCOL], F32)
        G_, B_, GC, GL, EG, EKD, EGL, EGS = range(8)
        sc = lambda i: SC.t[:, i, :]
        scr = SC.res[0]
        if not s.fz:
            P.dma("sp", s.ld, [], [scr], sc(G_), gtm)
            P.dma("sp", s.ld, [], [scr], sc(B_), btm)
        else:
            GH2 = 2 * cfg.GHV
            GB = Buf(P, "gbrows", [GH2, SEQ], F32)
            SEL = Buf(P, "gsel", [GH2, 2 * HVC], F32)
            gsem = P.new_dma_sem()
            P.dma("sp", gsem, [], [SEL.res[0]], SEL.t[:], s.din("gsel", [GH2, 2 * HVC]))
            for r in range(NCR):
                for ti in range(cfg.NT):
                    cs = 128 if ti == 0 else 0
                    d0 = r * cfg.MAIN + ti * T + cs - 128
                    P.dma("sp", gsem, [], [GB.res[0]], GB.t[:, d0:d0 + T - cs], g4[ti][r, :, cs:T])
            for n in range(NCH):
                bk = n % 8
                P.op("pe", [GB.res[0], SEL.res[0]], [s.PSR[bk]], lambda: nc.tensor.matmul(
                    s.PS[bk][:, 0:2 * HVC], lhsT=GB.t[:, n * 128:(n + 1) * 128], rhs=SEL.t[:], start=True, stop=True))
                P.op("act", [s.PSR[bk]], [scr], lambda: nc.scalar.copy(out=SC.t[:, B_, n * HVC:(n + 1) * HVC], in_=s.PS[bk][:, 0:HVC]))
                P.op("act", [s.PSR[bk]], [scr], lambda: nc.scalar.copy(out=SC.t[:, G_, n * HVC:(n + 1) * HVC], in_=s.PS[bk][:, HVC:2 * HVC]))
        for c0 in range(0, NCOL, 512):
            cn = min(512, NCOL - c0)
            b1, b2 = 0, 1
            P.op("pe", [scr, BMR], [s.PSR[b1]], lambda: nc.tensor.matmul(s.PS[b1][:, 0:cn], lhsT=bm(0), rhs=SC.t[:, G_, c0:c0 + cn], start=True, stop=True))
            P.op("pe", [scr] + s.cres, [s.PSR[b2]], lambda: nc.tensor.matmul(s.PS[b2][:, 0:cn], lhsT=s.ones_f, rhs=SC.t[:, G_, c0:c0 + cn], start=True, stop=True))
            P.op("act", [s.PSR[b1]], [scr], lambda: nc.scalar.copy(out=SC.t[:, GC, c0:c0 + cn], in_=s.PS[b1][:, 0:cn]))
            P.op("act", [s.PSR[b2]], [scr], lambda: nc.scalar.copy(out=SC.t[:, GL, c0:c0 + cn], in_=s.PS[b2][:, 0:cn]))
        P.op("act", [scr], [scr], lambda: nc.scalar.activation(out=sc(EG), in_=sc(GC), func=AF.Exp))
        P.op("act", [scr], [scr], lambda: nc.scalar.activation(out=sc(EGL), in_=sc(GL), func=AF.Exp))
        P.op("dve", [scr], [scr], lambda: nc.vector.tensor_tensor(out=sc(EKD), in0=sc(GL), in1=sc(GC), op=ALU.subtract))
        P.op("act", [scr], [scr], lambda: nc.scalar.activation(out=sc(EKD), in_=sc(EKD), func=AF.Exp))
        P.op("dve", [scr], [scr], lambda: nc.vector.tensor_scalar(out=sc(EGS), in0=sc(EG), scalar1=128.0 ** -0.5, scalar2=None, op0=ALU.mult))
        P.op("dve", [scr], [scr], lambda: nc.vector.tensor_scalar(out=sc(G_), in0=sc(B_), scalar1=-1.0, scalar2=None, op0=ALU.mult))
        NB_ = G_
        NU = 60
        upool = [[Buf(P, "u%d_%d" % (h, i), [128, 128], F32) for i in range(NU)] for h in range(HVC)]
        spool = [Buf(P, "sh%d" % i, [128, 128], F32) for i in range(24)]
        vpool = [Buf(P, "vt%d" % i, [128, 128], F32) for i in range(8)]
        opool = [Buf(P, "ot%d" % i, [128, 128], F32) for i in range(8)]
        for b in vpool + opool:
            b.dsem = P.new_dma_sem()
        cnt = {"s": 0, "b": 0, "v": 0, "o": 0}
        ucnt = [0] * HVC
        cur = [0]

        def ut():
            h = cur[0]
            b = upool[h][ucnt[h] % NU]
            ucnt[h] += 1
            return b

        def sht():
            b = spool[cnt["s"] % 24]
            cnt["s"] += 1
            return b

        def bank():
            b = cnt["b"] % 8
            cnt["b"] += 1
            return b

        hb = [0] * HVC
        assert HVC <= 4

        def hbank():
            h = cur[0]
            b = 2 * h + (hb[h] % 2)
            hb[h] += 1
            return b

        ldsem = [P.new_dma_sem() for _ in range(4)]
        S = [[Buf(P, "S%d_%d" % (h, i), [128, 128], F32) for i in range(2)] for h in range(HVC)]
        for h in range(HVC):
            P.op("dve", [], [S[h][0].res[0]], lambda h=h: nc.vector.memset(S[h][0].t[:], 0.0))
        out_res = []
        identr = s.cres

        def mm(out_bank, lhsT, rhs, reads):
            P.op("pe", reads, [s.PSR[out_bank]], lambda: nc.tensor.matmul(s.PS[out_bank][:, 0:128], lhsT=lhsT, rhs=rhs, start=True, stop=True))

        def tr(out_bank, in_, reads):
            P.op("pe", reads + identr, [s.PSR[out_bank]], lambda: nc.tensor.transpose(out=s.PS[out_bank][:, 0:128], in_=in_, identity=s.ident_f))

        def cp(dst, bk):
            P.op("act", [s.PSR[bk]], [dst.res[0]], lambda: nc.scalar.copy(out=dst.t[:], in_=s.PS[bk][:, 0:128]))

        def tt(dst, a, ar, b_, br, op):
            P.op("dve", ar + br, [dst.res[0]], lambda: nc.vector.tensor_tensor(out=dst.t[:], in0=a, in1=b_, op=op))

        def stt(dst, a, ar, scal, b_, br, op0, op1):
            P.op("dve", ar + br + [scr], [dst.res[0]], lambda: nc.vector.scalar_tensor_tensor(
                out=dst.t[:], in0=a, scalar=scal, in1=b_, op0=op0, op1=op1))

        def shared(n, hk):
            QT, KT = sht(), sht()
            sem = ldsem[n % 4]
            P.dma("sp", sem, [mres], [QT.res[0]], QT.t[:], q_src(n, hk, 0))
            P.dma("sp", sem, [mres], [KT.res[0]], KT.t[:], q_src(n, hk, 1))
            b1, b2, b3 = bank(), bank(), bank()
            mm(b1, KT.t[:], KT.t[:], [KT.res[0]])
            mm(b2, KT.t[:], QT.t[:], [KT.res[0], QT.res[0]])
            tr(b3, KT.t[:], [KT.res[0]])
            Gs, ARs, Ktm = sht(), sht(), sht()
            cp(Gs, b1)
            cp(ARs, b2)
            cp(Ktm, b3)
            return QT, KT, Gs, ARs, Ktm

        def unit(n, hv, sh, par):
            QT, KT, Gs, ARs, Ktm = sh
            col = n * HVC + hv
            c1 = lambda i: SC.t[:, i, col:col + 1]
            VT = vpool[cnt["v"] % 8]
            cnt["v"] += 1
            P.dma("sp", VT.dsem, [mres], [VT.res[0]], VT.t[:], v_src(n, hv))
            bv = hbank()
            tr(bv, VT.t[:], [VT.res[0]])
            Vtm = ut()
            cp(Vtm, bv)
            dg = ut()
            P.op("dve", identr + [scr], [dg.res[0]], lambda: nc.vector.tensor_scalar(
                out=dg.t[:], in0=s.ident_f, scalar1=c1(GC), scalar2=None, op0=ALU.mult))
            bR = hbank()
            mm(bR, s.ones_f, dg.t[:], [dg.res[0]] + identr)
            yield
            DnS, Dt = ut(), ut()
            stt(DnS, s.PS[bR][:, 0:128], [s.PSR[bR]], c1(GC), bm(1), [BMR], ALU.subtract, ALU.max)
            stt(Dt, s.PS[bR][:, 0:128], [s.PSR[bR]], c1(GC), bm(2), [BMR], ALU.subtract, ALU.min)
            P.op("act", [DnS.res[0]], [DnS.res[0]], lambda: nc.scalar.activation(out=DnS.t[:], in_=DnS.t[:], func=AF.Exp, scale=-1.0))
            P.op("act", [Dt.res[0]], [Dt.res[0]], lambda: nc.scalar.activation(out=Dt.t[:], in_=Dt.t[:], func=AF.Exp))
            An, At, attnT = ut(), ut(), ut()
            stt(An, Gs.t[:], [Gs.res[0]], c1(B_), DnS.t[:], [DnS.res[0]], ALU.mult, ALU.mult)
            bA = hbank()
            tr(bA, An.t[:], [An.res[0]])
            stt(attnT, ARs.t[:], [ARs.res[0]], 128.0 ** -0.5, Dt.t[:], [Dt.res[0]], ALU.mult, ALU.mult)
            cp(At, bA)
            yield
            E, Et = ut(), ut()
            tt(E, An.t[:], [An.res[0]], bm(3), [BMR], ALU.mult)
            tt(Et, At.t[:], [At.res[0]], bm(10), [BMR], ALU.mult)
            Tb, Tt = ut(), ut()
            tt(Tb, s.ident_f, identr, E.t[:], [E.res[0]], ALU.subtract)
            tt(Tt, s.ident_f, identr, Et.t[:], [Et.res[0]], ALU.subtract)
            for l in range(1, 7):
                last = (l == 6)
                E, Et = ut(), ut()
                tt(E, An.t[:], [An.res[0]], bm(3 + l), [BMR], ALU.mult)
                tt(Et, At.t[:], [At.res[0]], bm(10 + l), [BMR], ALU.mult)
                yield
                bM, bM2 = hbank(), hbank()
                if not last:
                    mm(bM, Et.t[:], Tb.t[:], [Et.res[0], Tb.res[0]])
                mm(bM2, E.t[:], Tt.t[:], [E.res[0], Tt.res[0]])
                M, M2 = ut(), ut()
                if not last:
                    cp(M, bM)
                cp(M2, bM2)
                yield
                bT, bT2 = hbank(), hbank()
                if not last:
                    mm(bT, Tt.t[:], M.t[:], [Tt.res[0], M.res[0]])
                mm(bT2, Tb.t[:], M2.t[:], [Tb.res[0], M2.res[0]])
                Tb2, Tt2 = ut(), ut()
                if not last:
                    tt(Tb2, Tb.t[:], [Tb.res[0]], s.PS[bT][:, 0:128], [s.PSR[bT]], ALU.subtract)
                tt(Tt2, Tt.t[:], [Tt.res[0]], s.PS[bT2][:, 0:128], [s.PSR[bT2]], ALU.subtract)
                Tb, Tt = Tb2, Tt2
                yield
            Sc, Sn = S[hv][par], S[hv][1 - par]
            bK = hbank()
            mm(bK, KT.t[:], Sc.t[:], [KT.res[0], Sc.res[0]])
            bQ = hbank()
            mm(bQ, QT.t[:], Sc.t[:], [QT.res[0], Sc.res[0]])
            t = ut()
            stt(t, s.PS[bK][:, 0:128], [s.PSR[bK]], c1(EG), Vtm.t[:], [Vtm.res[0]], ALU.mult, ALU.subtract)
            QSs = ut()
            cp(QSs, bQ)
            rv = ut()
            P.op("dve", [t.res[0], scr], [rv.res[0]], lambda: nc.vector.tensor_scalar(
                out=rv.t[:], in0=t.t[:], scalar1=c1(NB_), scalar2=None, op0=ALU.mult))
            yield
            bN = hbank()
            mm(bN, Tt.t[:], rv.t[:], [Tt.res[0], rv.res[0]])
            vn, vne = ut(), ut()
            cp(vn, bN)
            P.op("dve", [vn.res[0], scr], [vne.res[0]], lambda: nc.vector.tensor_scalar(
                out=vne.t[:], in0=vn.t[:], scalar1=c1(EKD), scalar2=None, op0=ALU.mult))
            yield
            bAV, bKV = hbank(), hbank()
            mm(bAV, attnT.t[:], vn.t[:], [attnT.res[0], vn.res[0]])
            mm(bKV, Ktm.t[:], vne.t[:], [Ktm.res[0], vne.res[0]])
            AVs = ut()
            cp(AVs, bAV)
            stt(Sn, Sc.t[:], [Sc.res[0]], c1(EGL), s.PS[bKV][:, 0:128], [s.PSR[bKV]], ALU.mult, ALU.add)
            o = ut()
            stt(o, QSs.t[:], [QSs.res[0]], c1(EGS), AVs.t[:], [AVs.res[0]], ALU.mult, ALU.add)
            yield
            sq, st = ut(), ut()
            P.op("act", [o.res[0]], [sq.res[0], st.res[0]], lambda: nc.scalar.activation(
                out=sq.t[:], in_=o.t[:], func=AF.Square, accum_out=st.t[:, 0:1]))
            P.op("act", [st.res[0], s.EPS.res[0]], [st.res[0]], lambda: nc.scalar.activation(
                out=st.t[:, 1:2], in_=st.t[:, 0:1], func=AF.Sqrt, scale=1.0 / 128, bias=s.EPS.t[:, 0:1]))
            P.op("dve", [st.res[0]], [st.res[0]], lambda: nc.vector.reciprocal(out=st.t[:, 2:3], in_=st.t[:, 1:2]))
            on = opool[cnt["o"] % 8]
            cnt["o"] += 1
            P.op("dve", [o.res[0], st.res[0], BMR], [on.res[0]], lambda: nc.vector.scalar_tensor_tensor(
                out=on.t[:], in0=o.t[:], scalar=st.t[:, 2:3], in1=bm(17), op0=ALU.mult, op1=ALU.mult))
            dres = Res()
            P.dma("sp", on.dsem, [on.res[0]], [dres], o_dst(n, hv), on.t[:])
            out_res.append(dres)

        for n in range(NCH):
            shs = [shared(n, hk) for hk in range(HKC)]
            gens = [(hv, unit(n, hv, shs[hv // REP], n % 2)) for hv in range(HVC)]
            while gens:
                for hv_, g in list(gens):
                    cur[0] = hv_
                    try:
                        next(g)
                    except StopIteration:
                        gens.remove((hv_, g))
            if s.fz and n % CPR == CPR - 1:
                p = n // CPR
                cres = Res()
                P.coll_allgather(NCR, list(out_res), [cres], o_loc[p].opt(), s.over["o_gat"][p].opt())
                out_res.clear()
                big = s.over["big"]
                gv = s.over["o_gat"][p].rearrange("(a q) c -> q a c", q=128)
                bv = big[p + 1].rearrange("(a q) c -> q a c", q=128)
                na = NCR * cfg.MAIN // 128
                for a0 in range(0, na, 8):
                    dres = Res()
                    P.dma("sp", s.iosem[p % 4], [cres], [dres], bv[:, a0:a0 + 8, :], gv[:, a0:a0 + 8, :])
                    s.parent.big_res.append(dres)
        s.finish(out_res)
        return s


class Mega(KB):
    def build(s):
        cfg, nc = s.cfg, s.nc
        NCR, T, NT = cfg.NCORE, cfg.T, cfg.NT
        P = s.P = Prog(nc, s.es)
        s.PS = [P.ps("ps%d" % i, [128, 512]) for i in range(8)]
        s.PSR = [Res(excl=True) for _ in range(8)]
        s.cid = nc.sync.partition_id()
        s.big_res = []
        dt = lambda name, shape: nc.dram_tensor(name, list(shape), F32).ap()
        x2s = dt("x2s", [cfg.D, cfg.TOK])
        zss = dt("zss", [cfg.GVD, cfg.TOK])
        HVh = cfg.HVC // 2
        rows = [cfg.GKD, cfg.GKD, NCR * HVh * 128, NCR * HVh * 128, 2 * cfg.GHV]
        gin = [[dt("gin%d_%d" % (ti, g), [rows[g], T]) for g in range(5)] for ti in range(NT)]
        gout = [[dt("gout%d_%d" % (ti, g), [NCR * rows[g], T]) for g in range(5)] for ti in range(NT)]
        o_loc = [dt("oloc%d" % p, [cfg.MAIN, cfg.HVC * 128]) for p in range(NCR)]
        o_gat = [dt("ogat%d" % p, [NCR * cfg.MAIN, cfg.HVC * 128]) for p in range(NCR)]
        big = dt("obig", [NCR + 1, NCR * cfg.MAIN, cfg.HVC * 128])
        xg = [cfg.HKC * 128, cfg.HKC * 128, HVh * 128, HVh * 128]
        mine = [[dt("mine%d_%d" % (ti, g), [NCR * xg[g], T]) for g in range(4)] for ti in range(NT)]
        mybig = nc.dram_tensor("mybig", [NCR, cfg.TOK, cfg.HVC * 128], F32).ap()
        def gather_tile(A_, ti):
            for g in range(5):
                P.coll_allgather(NCR, A_.grp_res[ti][g], [Res()], gin[ti][g].opt(), gout[ti][g].opt())
        s.gather_tile = gather_tile
        A = StageA(cfg, "a", parent=s)
        A.over = {"x2s": x2s, "zss": zss, "gin": gin}
        with A.es:
            A.build()
            z = Buf(P, "zz", [128, cfg.HVC * 128], F32)
            P.op("dve", [], [z.res[0]], lambda: nc.vector.memset(z.t[:], 0.0))
            bz = big[0].rearrange("(r q) c -> r q c", r=NCR)
            for r in range(NCR):
                dres = Res()
                P.dma("sp", A.iosem[0], [z.res[0]], [dres], bz[r, cfg.MAIN - 128:cfg.MAIN, :], z.t[:])
            s.gather_tile(A, NT - 1)
            P.barrier()
        B = StageB(cfg, "b", parent=s)
        B.over = {"gout": gout, "o_loc": o_loc, "o_gat": o_gat, "big": big, "mine": mine}
        with B.es:
            B.build()
            P.barrier()
        C = StageC(cfg, "c", parent=s)
        C.over = {"x2T": x2s, "zsT": zss, "big": big, "mybig": mybig}
        with C.es:
            C.build()
        return s


def fm(v, n=128):
    v = np.asarray(v, np.float32)
    return np.ascontiguousarray(v.reshape(-1, n).T)


def wt(W):
    W = np.asarray(W, np.float32)
    K_, N_ = W.shape
    return np.ascontiguousarray(W.reshape(K_ // 128, 128, N_ // 128, 128).transpose(2, 1, 0, 3))


def tiled_c(inp):
    out = {"b_w_o": wt(inp["b_w_o"][0]), "c_w_pw1": wt(inp["c_w_pw1"][0]), "c_w_pw2": wt(inp["c_w_pw2"][0]),
           "d_w_in": wt(inp["d_w_in"][0]), "d_w_out": wt(inp["d_w_out"][0])}
    for l in (1, 2, 3):
        out["wg%d" % l] = wt(inp["f_w_gate"][l])
        out["wu%d" % l] = wt(inp["f_w_up"][l])
        out["wd%d" % l] = wt(inp["f_w_down"][l])
    return out


def make_consts():
    c = np.zeros((128, 4, 128), np.float32)
    c[:, 0, :] = 1.0
    c[:, 1, :] = np.eye(128, dtype=np.float32)
    for m in range(128):
        c[(m + 64) % 128, 2, m] = 1.0
    return c


def pack_vecs(cfg, inp):
    V = np.zeros((128, NVEC(cfg), cfg.KC), np.float32)
    for l in range(4):
        V[:, V_MIX0 + 2 * l] = fm(inp["mix_norm"][l])
        V[:, V_FFN0 + 2 * l] = fm(inp["ffn_norm"][l])
    V[:, V_FINAL] = fm(inp["final_norm"])
    D = cfg.D
    V[:, V_CB1V] = fm(inp["c_b_pw1"][0][:D])
    V[:, V_CB1G] = fm(inp["c_b_pw1"][0][D:])
    V[:, V_CBDW] = fm(inp["c_b_dw"][0])
    V[:, V_CLNG] = fm(inp["c_ln_g"][0])
    V[:, V_CLNB] = fm(inp["c_ln_b"][0])
    V[:, V_CB2] = fm(inp["c_b_pw2"][0])
    for k in range(3):
        V[:, V_DCONV + k] = fm(inp["d_w_conv"][0][k])
    for k in range(cfg.CK):
        V[:, V_CDW + k] = fm(inp["c_w_dw"][0][k])
    return V


def pack_fvecs(cfg, inp, layers):
    Fv = np.zeros((128, 4 * len(layers), cfg.FC), np.float32)
    for i, l in enumerate(layers):
        for k in range(3):
            Fv[:, 4 * i + k] = fm(inp["f_w_conv"][l][k])
        Fv[:, 4 * i + 3] = fm(inp["f_b_conv"][l])
    return Fv


def halo_slices(cfg, aT, c):
    s0 = c * cfg.MAIN - cfg.HALO
    if s0 >= 0:
        return np.ascontiguousarray(aT[:, s0:s0 + cfg.TOK])
    out = np.zeros((aT.shape[0], cfg.TOK), aT.dtype)
    out[:, -s0:] = aT[:, 0:cfg.TOK + s0]
    return out


def hv_arr(c):
    return np.full((128, 1), 0.0 if c == 0 else 1.0, np.float32)


def run_stage_c(cfg, inp, x2T, onT, zsT, trace=False):
    st = StageC(cfg, "c")
    with st.es:
        st.build()
    V = pack_vecs(cfg, inp)
    Fv = pack_fvecs(cfg, inp, [1, 2, 3])
    cst = make_consts()
    TWC = tiled_c(inp)
    maps = []
    for c in range(cfg.NCORE):
        m = {"consts": cst, "vecs": V, ("fvecs_%d" % (Fv.shape[1] // 4)): Fv, "hv_in": hv_arr(c),
             "x2T": halo_slices(cfg, x2T, c), "onT": halo_slices(cfg, onT, c), "zsT": halo_slices(cfg, zsT, c),
             }
        m.update(TWC)
        maps.append(m)
    res = run_bass_kernel_spmd(st.nc, maps, core_ids=list(range(cfg.NCORE)), trace=trace)
    outT = np.concatenate([r["outT"] for r in res.results], axis=1)
    return outT, res


def rope_tables(cfg, c):
    n = 128 + cfg.TOK
    pos = (c * cfg.MAIN - 256 + np.arange(n)).astype(np.float32)
    inv = np.power(np.float32(10000.0), -np.arange(64, dtype=np.float32) / np.float32(64)).astype(np.float32)
    ang = (pos[None, :] * inv[:, None]).astype(np.float32)
    cs, sn = np.cos(ang).astype(np.float32), np.sin(ang).astype(np.float32)
    tab = np.zeros((128, 2, n), np.float32)
    tab[0:64, 0], tab[64:128, 0] = cs, cs
    tab[0:64, 1], tab[64:128, 1] = -sn, sn
    tab[:, :, pos < 0] = 0.0
    return tab


def attn_masks(c):
    q = np.arange(128)[:, None]
    k = np.arange(128)[None, :]
    NEG = np.float32(-30000.0)
    m = np.zeros((128, 2, 256), np.float32)
    prev = np.where(k > q, 0.0, NEG).astype(np.float32)
    cur = np.where(k <= q, 0.0, NEG).astype(np.float32)
    m[:, 0, 0:128], m[:, 0, 128:256] = prev, cur
    m[:, 1, 0:128], m[:, 1, 128:256] = (NEG if c == 0 else prev), cur
    return m


def run_stage_a(cfg, inp, xT, trace=False):
    st = StageA(cfg, "a")
    with st.es:
        st.build()
    V = pack_vecs(cfg, inp)
    Fv = pack_fvecs(cfg, inp, [0])
    cst = make_consts()
    NQK = 2 * cfg.GHK + cfg.GHV
    gconv = np.stack([fm(inp["b_conv"][0][k]) for k in range(4)], axis=1)
    gsc = np.stack([inp["b_dt_bias"][0], inp["b_a_log"][0]], axis=1).astype(np.float32)
    sinkb = np.ascontiguousarray(np.broadcast_to(inp["a_sinks"][0][None, :], (128, cfg.AH))).astype(np.float32)
    maps = []
    for c in range(cfg.NCORE):
        s0 = c * cfg.MAIN - 256
        n = 128 + cfg.TOK
        xc = np.zeros((cfg.D, n), np.float32)
        lo = max(0, s0)
        xc[:, lo - s0:] = xT[:, lo:s0 + n]
        maps.append({"consts": cst, "vecs": V, ("fvecs_%d" % (Fv.shape[1] // 4)): Fv, "hv_in": hv_arr(c), "xT": xc,
                     "a_w_qkv": inp["a_w_qkv"][0], "a_w_o": inp["a_w_o"][0], "wg0": inp["f_w_gate"][0],
                     "wu0": inp["f_w_up"][0], "wd0": inp["f_w_down"][0], "b_w_in": inp["b_w_in"][0],
                     "rope_tab": rope_tables(cfg, c), "masks": attn_masks(c), "sinkb": sinkb, "gconv": gconv, "gsc": gsc})
    res = run_bass_kernel_spmd(st.nc, maps, core_ids=list(range(cfg.NCORE)), trace=trace)
    out = {k: np.concatenate([r[k] for r in res.results], axis=1) for k in ("x2T", "qkT", "vT", "zsT", "gbT")}
    return out, res


def gdn_masks(norm_w):
    bmk = np.zeros((128, NBM, 128), np.float32)
    i = np.arange(128)[:, None]
    j = np.arange(128)[None, :]
    bmk[:, 0] = (i <= j)
    bmk[:, 1] = np.where(i > j, 0.0, 1e4)
    bmk[:, 2] = np.where(j >= i, 0.0, -1e4)
    for l in range(7):
        b = 1 << l
        mN = ((i // (2 * b)) == (j // (2 * b))) & (((i // b) % 2) == 1) & (((j // b) % 2) == 0)
        bmk[:, 3 + l] = mN
        bmk[:, 10 + l] = mN.T
    bmk[:, 17] = np.broadcast_to(np.asarray(norm_w, np.float32)[None, :], (128, 128))
    return bmk


def run_stage_b(cfg, inp, qkT, vT, gbT, trace=False):
    st = StageB(cfg, "b")
    with st.es:
        st.build()
    cst = make_consts()
    bmk = gdn_masks(inp["b_norm"][0])
    HVC, HKC, NCH, GHK, GHV = cfg.HVC, cfg.HKC, cfg.NCH, cfg.GHK, cfg.GHV
    maps = []
    for c in range(cfg.NCORE):
        qs = qkT[(c * HKC) * 128:((c + 1) * HKC) * 128]
        ks = qkT[(GHK + c * HKC) * 128:(GHK + (c + 1) * HKC) * 128]
        vs = vT[(c * HVC) * 128:((c + 1) * HVC) * 128]
        bt = gbT[c * HVC:(c + 1) * HVC]
        gt = gbT[GHV + c * HVC:GHV + (c + 1) * HVC]
        tm = lambda a: np.ascontiguousarray(a.reshape(HVC, NCH, 128).transpose(2, 1, 0).reshape(128, NCH * HVC))
        maps.append({"consts": cst, "bmasks": bmk, "qT": np.ascontiguousarray(qs), "kT": np.ascontiguousarray(ks),
                     "vT": np.ascontiguousarray(vs), "gtm": tm(gt), "btm": tm(bt)})
    res = run_bass_kernel_spmd(st.nc, maps, core_ids=list(range(cfg.NCORE)), trace=trace)
    o = np.concatenate([r["o_tm"] for r in res.results], axis=1)
    return o, res


FUSED = True
def run_mega(cfg, inp, xT, trace=False):
    mg = Mega(cfg, "m")
    with mg.es:
        mg.build()
    V = pack_vecs(cfg, inp)
    FvA = pack_fvecs(cfg, inp, [0])
    FvC = pack_fvecs(cfg, inp, [1, 2, 3])
    cst = make_consts()
    gconv = np.stack([fm(inp["b_conv"][0][k]) for k in range(4)], axis=1)
    gsc = np.stack([inp["b_dt_bias"][0], inp["b_a_log"][0]], axis=1).astype(np.float32)
    sinkb = np.ascontiguousarray(np.broadcast_to(inp["a_sinks"][0][None, :], (128, cfg.AH))).astype(np.float32)
    bmk = gdn_masks(inp["b_norm"][0])
    TWC = tiled_c(inp)
    maps = []
    for c in range(cfg.NCORE):
        s0 = c * cfg.MAIN - 256
        n = 128 + cfg.TOK
        xc = np.zeros((cfg.D, n), np.float32)
        lo = max(0, s0)
        xc[:, lo - s0:] = xT[:, lo:s0 + n]
        m = {"consts": cst, "vecs": V, "fvecs_1": FvA, "fvecs_3": FvC, "hv_in": hv_arr(c), "xT": xc,
             "a_w_qkv": inp["a_w_qkv"][0], "a_w_o": inp["a_w_o"][0], "wg0": inp["f_w_gate"][0],
             "wu0": inp["f_w_up"][0], "wd0": inp["f_w_down"][0], "b_w_in": inp["b_w_in"][0],
             "rope_tab": rope_tables(cfg, c), "masks": attn_masks(c), "sinkb": sinkb, "gconv": gconv, "gsc": gsc,
             "bmasks": bmk, "gsel": gsel_arr(cfg, c),
             }
        m.update(TWC)
        maps.append(m)
    res = run_bass_kernel_spmd(mg.nc, maps, core_ids=list(range(cfg.NCORE)), trace=trace)
    outT = np.concatenate([r["outT"] for r in res.results], axis=1)
    return outT, res


def kernel(**inputs):
    inp = {k: np.asarray(v) for k, v in inputs.items()}
    cfg = Cfg()
    if FUSED:
        xT = np.ascontiguousarray(inp["x"][0].T)
        outT, _ = run_mega(cfg, inp, xT)
        return np.ascontiguousarray(outT.T)[None].astype(np.float32)
    xT = np.ascontiguousarray(inp["x"][0].T)
    a, _ = run_stage_a(cfg, inp, xT)
    o_tm, _ = run_stage_b(cfg, inp, a["qkT"], a["vT"], a["gbT"])
    onT = np.ascontiguousarray(o_tm.T)
    outT, _ = run_stage_c(cfg, inp, a["x2T"], onT, a["zsT"])
    return np.ascontiguousarray(outT.T)[None].astype(np.float32)


def gsel_arr(cfg, c):
    m = np.zeros((2 * cfg.GHV, 2 * cfg.HVC), np.float32)
    for j in range(cfg.HVC):
        m[c * cfg.HVC + j, j] = 1.0
        m[cfg.GHV + c * cfg.HVC + j, cfg.HVC + j] = 1.0
    return m
```
